# Optimizing a Trainium2 kernel written in Bass

```python
import math
import jax, jax.numpy as jnp
from jax import lax
import numpy as np

D_MODEL = 2048
BATCH = 4
SEQ = 2048
DEPTH = 1

GRID_W = 64
CTX_LEN = 256
MIX_WIDTH = D_MODEL
POOL_WIDTH = MIX_WIDTH // 2
POOL_WINDOWS = (2, 4, 8, 16)
POOL_GROUP = POOL_WIDTH // len(POOL_WINDOWS)
ATTN_WIDTH = MIX_WIDTH - POOL_WIDTH
N_HEADS = 8
V_HEAD_DIM = ATTN_WIDTH // N_HEADS
QK_HEAD_DIM = V_HEAD_DIM // 2
ROPE_AXIS_DIM = QK_HEAD_DIM // 2
ROPE_BASE = 10000.0
D_FF = -(-8 * D_MODEL // 768) * 256
Q_BLOCK = 128
IN_WIDTH = POOL_WIDTH + 3 * ATTN_WIDTH
N_MOD = 6
EPS = 1e-6

kernel_name = "hybrid_pool_diffattn_dit_block"


def rmsnorm(x, g):
    xf = x.astype(jnp.float32)
    y = xf * lax.rsqrt(jnp.mean(xf * xf, axis=-1, keepdims=True) + EPS)
    return (y * g.astype(jnp.float32)).astype(x.dtype)


def axial_rope_tables(n_tokens):
    rows = n_tokens // GRID_W
    row = jnp.repeat(jnp.arange(rows), GRID_W)
    col = jnp.tile(jnp.arange(GRID_W), rows)
    half = ROPE_AXIS_DIM // 2
    inv = ROPE_BASE ** (-jnp.arange(half, dtype=jnp.float32) / half)
    ang_r = row.astype(jnp.float32)[:, None] * inv
    ang_c = col.astype(jnp.float32)[:, None] * inv
    return (jnp.cos(ang_r)[None, :, None, :], jnp.sin(ang_r)[None, :, None, :],
            jnp.cos(ang_c)[None, :, None, :], jnp.sin(ang_c)[None, :, None, :])


def _rotate(x, cos, sin):
    x1, x2 = jnp.split(x, 2, axis=-1)
    return jnp.concatenate([x1 * cos - x2 * sin, x1 * sin + x2 * cos], axis=-1)


def apply_axial_rope(x, tables):
    cr, sr, cc, sc = tables
    xf = x.astype(jnp.float32)
    out = jnp.concatenate([_rotate(xf[..., :ROPE_AXIS_DIM], cr, sr),
                           _rotate(xf[..., ROPE_AXIS_DIM:], cc, sc)], axis=-1)
    return out.astype(x.dtype)


def pool_mixer(u, w_pool_l, pool_scale_l):
    b, n = u.shape[0], u.shape[1]
    uf = u.astype(jnp.float32)
    cs = jnp.concatenate([jnp.zeros((b, 1, POOL_WIDTH), jnp.float32),
                          jnp.cumsum(uf, axis=1)], axis=1)
    t = jnp.arange(n)
    outs = []
    for g, w in enumerate(POOL_WINDOWS):
        lo = jnp.clip(t - w // 2, 0, n)
        hi = jnp.clip(t + w // 2, 0, n)
        sl = slice(g * POOL_GROUP, (g + 1) * POOL_GROUP)
        seg = cs[:, :, sl]
        mean = (seg[:, hi] - seg[:, lo]) / (hi - lo).astype(jnp.float32)[None, :, None]
        d = (mean - uf[:, :, sl]).astype(u.dtype)
        outs.append(jnp.einsum('blc,cd->bld', d, w_pool_l[g]))
    return jnp.concatenate(outs, axis=-1) * pool_scale_l


def project(h, w_in_l):
    b, n = h.shape[0], h.shape[1]
    proj = h @ w_in_l
    p = proj[..., :POOL_WIDTH]
    q = proj[..., POOL_WIDTH:POOL_WIDTH + ATTN_WIDTH].reshape(b, n, N_HEADS, 2, QK_HEAD_DIM)
    k = proj[..., POOL_WIDTH + ATTN_WIDTH:POOL_WIDTH + 2 * ATTN_WIDTH].reshape(b, n, N_HEADS, 2, QK_HEAD_DIM)
    v = proj[..., POOL_WIDTH + 2 * ATTN_WIDTH:].reshape(b, n, N_HEADS, V_HEAD_DIM)
    return p, q[..., 0, :], q[..., 1, :], k[..., 0, :], k[..., 1, :], v


def diff_attend(q1, q2, k1, k2, v, lam):
    scale = QK_HEAD_DIM ** -0.5
    s1 = jnp.einsum('bqhd,bkhd->bhqk', q1, k1).astype(jnp.float32) * scale
    s2 = jnp.einsum('bqhd,bkhd->bhqk', q2, k2).astype(jnp.float32) * scale
    p = jax.nn.softmax(s1, axis=-1) - lam * jax.nn.softmax(s2, axis=-1)
    return jnp.einsum('bhqk,bkhe->bqhe', p.astype(v.dtype), v)


def diff_attend_blocked(q1, q2, k1, k2, v, lam):
    b, n = q1.shape[0], q1.shape[1]
    nb = n // Q_BLOCK

    def to_blocks(q):
        return q.reshape(b, nb, Q_BLOCK, N_HEADS, QK_HEAD_DIM).swapaxes(0, 1)

    out = lax.map(lambda qs: diff_attend(qs[0], qs[1], k1, k2, v, lam),
                  (to_blocks(q1), to_blocks(q2)))
    return out.swapaxes(0, 1).reshape(b, n, N_HEADS, V_HEAD_DIM)


def swiglu(h, w_gate_up_l, w_down_l):
    gu = h @ w_gate_up_l
    g, u = jnp.split(gu, 2, axis=-1)
    return (jax.nn.silu(g) * u) @ w_down_l


def setup_inputs(seed: int = 0) -> dict:
    key = jax.random.key(seed)
    ks = jax.random.split(key, 24)
    f32 = jnp.float32

    def nrm(k, shape, scale):
        return jax.random.normal(k, shape, f32) * scale

    def gain(k, shape):
        return 1.0 + 0.02 * jax.random.normal(k, shape, f32)

    return {
        "x": nrm(ks[0], (BATCH, SEQ, D_MODEL), 1.0),
        "c": nrm(ks[1], (BATCH, D_MODEL), 1.0),
        "ctx": nrm(ks[2], (BATCH, CTX_LEN, D_MODEL), 1.0),
        "c_ctx": nrm(ks[3], (D_MODEL,), 1.0),
        "w_ada": nrm(ks[4], (DEPTH, D_MODEL, N_MOD * D_MODEL), D_MODEL ** -0.5),
        "b_ada": nrm(ks[5], (DEPTH, N_MOD * D_MODEL), 0.01),
        "g_pre_mix": gain(ks[6], (DEPTH, D_MODEL)),
        "g_post_mix": gain(ks[7], (DEPTH, D_MODEL)),
        "g_pre_ffn": gain(ks[8], (DEPTH, D_MODEL)),
        "g_post_ffn": gain(ks[9], (DEPTH, D_MODEL)),
        "w_in": nrm(ks[10], (DEPTH, D_MODEL, IN_WIDTH), D_MODEL ** -0.5),
        "w_pool": nrm(ks[11], (DEPTH, len(POOL_WINDOWS), POOL_GROUP, POOL_GROUP), POOL_GROUP ** -0.5),
        "pool_scale": gain(ks[12], (DEPTH, POOL_WIDTH)),
        "lambda_q1": nrm(ks[13], (DEPTH, QK_HEAD_DIM), 0.1),
        "lambda_k1": nrm(ks[14], (DEPTH, QK_HEAD_DIM), 0.1),
        "lambda_q2": nrm(ks[15], (DEPTH, QK_HEAD_DIM), 0.1),
        "lambda_k2": nrm(ks[16], (DEPTH, QK_HEAD_DIM), 0.1),
        "g_subln": gain(ks[17], (DEPTH, V_HEAD_DIM)),
        "w_out": nrm(ks[18], (DEPTH, MIX_WIDTH, D_MODEL), MIX_WIDTH ** -0.5),
        "w_gate_up": nrm(ks[19], (DEPTH, D_MODEL, 2 * D_FF), D_MODEL ** -0.5),
        "w_down": nrm(ks[20], (DEPTH, D_FF, D_MODEL), D_FF ** -0.5),
    }


def reference(x, c, ctx, c_ctx, w_ada, b_ada, g_pre_mix, g_post_mix, g_pre_ffn, g_post_ffn,
              w_in, w_pool, pool_scale, lambda_q1, lambda_k1, lambda_q2, lambda_k2, g_subln,
              w_out, w_gate_up, w_down):
    b, n = x.shape[0], x.shape[1]
    tables = axial_rope_tables(n)
    silu_c = jax.nn.silu(c)
    silu_cc = jax.nn.silu(c_ctx)

    for l in range(DEPTH):
        lambda_init = 0.8 - 0.6 * math.exp(-0.3 * l)
        lam = (jnp.exp(jnp.sum(lambda_q1[l].astype(jnp.float32) * lambda_k1[l].astype(jnp.float32)))
               - jnp.exp(jnp.sum(lambda_q2[l].astype(jnp.float32) * lambda_k2[l].astype(jnp.float32)))
               + lambda_init)

        mod = silu_c @ w_ada[l] + b_ada[l]
        sh_m, sc_m, gt_m, sh_f, sc_f, gt_f = jnp.split(mod[:, None, :], N_MOD, axis=-1)
        mod_c = silu_cc @ w_ada[l] + b_ada[l]
        shc_m, scc_m, gtc_m, shc_f, scc_f, gtc_f = jnp.split(mod_c, N_MOD, axis=-1)

        h = rmsnorm(x, g_pre_mix[l]) * (1 + sc_m) + sh_m
        hc = rmsnorm(ctx, g_pre_mix[l]) * (1 + scc_m) + shc_m
        p, q1, q2, k1, k2, v = project(h, w_in[l])
        pc, q1c, q2c, k1c, k2c, vc = project(hc, w_in[l])

        q1 = apply_axial_rope(q1, tables)
        q2 = apply_axial_rope(q2, tables)
        k1 = apply_axial_rope(k1, tables)
        k2 = apply_axial_rope(k2, tables)
        k1_all = jnp.concatenate([k1c, k1], axis=1)
        k2_all = jnp.concatenate([k2c, k2], axis=1)
        v_all = jnp.concatenate([vc, v], axis=1)

        attn = diff_attend_blocked(q1, q2, k1_all, k2_all, v_all, lam)
        attn = rmsnorm(attn, g_subln[l]) * (1 - lambda_init)
        pool = pool_mixer(p, w_pool[l], pool_scale[l])
        mix = jnp.concatenate([pool, attn.reshape(b, n, ATTN_WIDTH)], axis=-1) @ w_out[l]
        x_new = x + gt_m * rmsnorm(mix, g_post_mix[l])

        h2 = rmsnorm(x_new, g_pre_ffn[l]) * (1 + sc_f) + sh_f
        x_new = x_new + gt_f * rmsnorm(swiglu(h2, w_gate_up[l], w_down[l]), g_post_ffn[l])

        if l < DEPTH - 1:
            attn_c = diff_attend(q1c, q2c, k1c, k2c, vc, lam)
            attn_c = rmsnorm(attn_c, g_subln[l]) * (1 - lambda_init)
            pool_c = pool_mixer(pc, w_pool[l], pool_scale[l])
            mix_c = jnp.concatenate([pool_c, attn_c.reshape(b, ctx.shape[1], ATTN_WIDTH)], axis=-1) @ w_out[l]
            ctx = ctx + gtc_m * rmsnorm(mix_c, g_post_mix[l])
            h2c = rmsnorm(ctx, g_pre_ffn[l]) * (1 + scc_f) + shc_f
            ctx = ctx + gtc_f * rmsnorm(swiglu(h2c, w_gate_up[l], w_down[l]), g_post_ffn[l])

        x = x_new

    return x
```

```python
import math
import numpy as np
import ml_dtypes
import concourse.bass as bass
import concourse.mybir as mybir
from concourse.bass_utils import run_bass_kernel_spmd

F32 = mybir.dt.float32
BF16 = mybir.dt.bfloat16
AF = mybir.ActivationFunctionType
ALU = mybir.AluOpType

D = 2048
SEQ = 2048
CTX = 256
NOWN = 1024
DFF = 5632
EPS = 1e-6
NCORES = 8
LAMBDA_INIT = 0.8 - 0.6 * math.exp(-0.3 * 0)
QSCALE = 64 ** -0.5

C_C, C_CC, C_B0, C_B1, C_B3, C_B4, C_GPM, C_GPF, C_PS, C_HL, C_HR = 0, 16, 32, 48, 64, 80, 96, 112, 128, 136, 137
NCOL = 144
R_B2, R_B5, R_GPOM, R_GPOF, R_GSUB, R_LAM, R_INVC = 0, 2048, 4096, 6144, 8192, 8320, 8576
NROW = 8640

DEBUG = False


class Prog:
    CE = ("pe", "act", "dve", "pool")

    def __init__(self, n_dma_sp=12, n_dma_pool=4):
        self.ops = {e: [] for e in ("pe", "act", "dve", "pool", "sp")}
        self.cnt = {e: 0 for e in self.CE}
        self.unsig = {e: False for e in self.CE}
        self.res = {}
        self.waited = {e: {} for e in self.ops}
        self.pending = {e: {} for e in self.ops}
        self.dq = {"sp": [["dsp%d" % i, 0] for i in range(n_dma_sp)],
                   "pool": [["dpl%d" % i, 0] for i in range(n_dma_pool)]}
        self.rr = {"sp": 0, "pool": 0}

    @staticmethod
    def _merge(dst, src):
        for k, v in src.items():
            if dst.get(k, 0) < v:
                dst[k] = v

    def _deps(self, eng, reads, writes):
        raw, oth = {}, {}
        for k in reads:
            r = self.res.get(k)
            if r:
                self._merge(raw, r[0])
        for k in writes:
            r = self.res.get(k)
            if r:
                self._merge(oth, r[0])
                self._merge(oth, r[1])
        if eng == "pe":
            oth.pop(eng, None)
            raw.pop(eng, None)
        deps = dict(self.pending[eng])
        self.pending[eng] = {}
        self._merge(deps, raw)
        self._merge(deps, oth)
        return deps

    def _emit(self, eng, deps, fn, inc):
        waits = []
        w = self.waited[eng]
        for k, v in deps.items():
            if w.get(k, 0) < v:
                w[k] = v
                waits.append((k, v))
        self.ops[eng].append((waits, fn, inc))

    def _register(self, tok, reads, writes):
        for k in reads:
            r = self.res.setdefault(k, [{}, {}])
            self._merge(r[1], {tok[0]: tok[1]})
        for k in writes:
            self.res[k] = [{tok[0]: tok[1]}, {}]

    def op(self, eng, fn, reads=(), writes=(), signal=True):
        deps = self._deps(eng, reads, writes)
        if signal:
            self.cnt[eng] += 1
            tok = (eng, self.cnt[eng])
            self.unsig[eng] = False
            inc = (eng, 1)
        else:
            tok = (eng, self.cnt[eng] + 1)
            if reads or writes:
                self.unsig[eng] = True
            inc = None
        self._emit(eng, deps, fn, inc)
        self._register(tok, reads, writes)
        return tok

    def dma(self, q, fn, reads=(), writes=()):
        slot = self.dq[q][self.rr[q] % len(self.dq[q])]
        self.rr[q] += 1
        deps = self._deps(q, reads, writes)
        if slot[1] > 0:
            self._merge(deps, {slot[0]: slot[1]})
        slot[1] += 16
        tok = (slot[0], slot[1])
        self._emit(q, deps, fn, (slot[0], 16))
        self._register(tok, reads, writes)
        return tok

    def all_counts(self):
        d = {}
        for e in self.CE:
            assert not self.unsig[e], "engine %s has unsignaled tail" % e
            if self.cnt[e]:
                d[e] = self.cnt[e]
        for q in self.dq.values():
            for k, v in q:
                if v:
                    d[k] = v
        return d

    def barrier(self):
        d = self.all_counts()
        for e in self.ops:
            self._merge(self.pending[e], d)
        self.res = {}

    def barrier_with(self, d, keep_keys):
        kept = {k: self.res[k] for k in keep_keys if k in self.res}
        for e in self.ops:
            self._merge(self.pending[e], d)
        self.res = kept

    def sem_names(self):
        return list(self.CE) + [s[0] for q in self.dq.values() for s in q]


def build_nc():
    nc = bass.Bass("TRN2", target_bir_lowering=False)
    P = Prog()

    xa = nc.dram_tensor("xa", [19 * 128, D], F32, kind="ExternalInput").ap()
    cols_d = nc.dram_tensor("cols", [128, NCOL], F32, kind="ExternalInput").ap()
    rows_d = nc.dram_tensor("rows", [NROW], F32, kind="ExternalInput").ap()
    rope_d = nc.dram_tensor("rope", [16, 128, 128], F32, kind="ExternalInput").ap()
    ident_d = nc.dram_tensor("ident", [128, 128], BF16, kind="ExternalInput").ap()
    w_ada = nc.dram_tensor("w_ada", [D, 6 * D], F32, kind="ExternalInput").ap()
    w_in = nc.dram_tensor("w_in", [D, 4096], F32, kind="ExternalInput").ap()
    w_pool = nc.dram_tensor("w_pool", [4, 256, 256], F32, kind="ExternalInput").ap()
    w_out = nc.dram_tensor("w_out", [D, D], F32, kind="ExternalInput").ap()
    w_gu = nc.dram_tensor("w_gu", [D, 2 * DFF], F32, kind="ExternalInput").ap()
    w_dn = nc.dram_tensor("w_dn", [DFF, D], F32, kind="ExternalInput").ap()
    out_d = nc.dram_tensor("out", [NOWN, D], F32, kind="ExternalOutput").ap()
    xscr = nc.dram_tensor("xscr", [NOWN, D], F32, kind="Internal").ap()
    dbg = {}
    if DEBUG:
        dbg["hT"] = nc.dram_tensor("dbg_hT", [128, 16 * 1024], BF16, kind="ExternalOutput").ap()
        dbg["KT"] = nc.dram_tensor("dbg_KT", [128, 8 * 2304], BF16, kind="ExternalOutput").ap()
        dbg["V"] = nc.dram_tensor("dbg_V", [128, 18 * 8 * 130], BF16, kind="ExternalOutput").ap()
        dbg["QT"] = nc.dram_tensor("dbg_QT", [128, 8 * 1024], BF16, kind="ExternalOutput").ap()
        dbg["mixT"] = nc.dram_tensor("dbg_mixT", [128, 16 * 1024], BF16, kind="ExternalOutput").ap()
        dbg["modc"] = nc.dram_tensor("dbg_modc", [128, 96], F32, kind="ExternalOutput").ap()
        dbg["h2T"] = nc.dram_tensor("dbg_h2T", [128, 16 * 1024], BF16, kind="ExternalOutput").ap()
        dbg["GT"] = nc.dram_tensor("dbg_GT", [128, 2 * 2048], F32, kind="ExternalOutput").ap()

    w_ada_v = w_ada.rearrange("(kc p) n -> p kc n", p=128)
    w_in_v = w_in.rearrange("(kc p) n -> p kc n", p=128)
    w_out_v = w_out.rearrange("(kc p) n -> p kc n", p=128)
    w_gu_v = w_gu.rearrange("(kc p) n -> p kc n", p=128)
    w_dn_v = w_dn.rearrange("(kc p) n -> p kc n", p=128)
    w_pool_v = w_pool.rearrange("g (kc p) n -> p g kc n", p=128)

    ARENA = 206912
    import contextlib
    es = contextlib.ExitStack()
    with es:
        arena = es.enter_context(nc.sbuf_tensor("arena", [128, ARENA // 2], BF16))
        ps = es.enter_context(nc.psum_tensor("ps", [128, 4096], F32))
        ident = es.enter_context(nc.sbuf_tensor("identsb", [128, 128], BF16))
        cols = es.enter_context(nc.sbuf_tensor("colsb", [128, NCOL], F32))
        modc = es.enter_context(nc.sbuf_tensor("modcsb", [128, 96], F32))
        sml = es.enter_context(nc.sbuf_tensor("sml", [128, 256], F32))
        cpair = es.enter_context(nc.sbuf_tensor("cpair", [128, 16, 2], BF16))
        sml2 = es.enter_context(nc.sbuf_tensor("sml2", [128, 512], F32))
        sems = {n: es.enter_context(nc.semaphore(n)) for n in P.sem_names()}

        def region(off, nelem, dt):
            assert off % 4 == 0
            if dt == BF16:
                assert off + 2 * nelem <= ARENA, (off, nelem)
                return arena[:, off // 2: off // 2 + nelem]
            assert off + 4 * nelem <= ARENA, (off, nelem)
            return arena[:, off // 2: off // 2 + 2 * nelem].bitcast(F32)

        def bank(b, n=1):
            return ps[:, b * 512:(b + n) * 512]

        def bankbf(b, n=1):
            return ps[:, b * 512:(b + n) * 512].bitcast(BF16)

        S_SC = 0
        S_SS = 32
        S_LN = 52
        S_RS = 72
        S_LAM = 92
        S_RR = 100
        S_SSA = 108
        S_LNA = 112
        S_RSA = 116
        S_TMP = 120
        S_SS2 = 152
        S_LN2 = 160
        S_RS2 = 168
        S_SS3 = 176
        S_LN3 = 184
        S_RS3 = 192
        S_SS4 = 200
        S_LN4 = 208
        S_RS4 = 216

        RING = 0
        SLOT = 16384
        O_KT = 65536
        O_V = O_KT + 36864
        O_HT = O_V + 37440
        O_HALO = O_HT + 32768
        O_X = O_HALO + 512

        def ring_slot(s, shape_str=None, **kw):
            v = region(RING + s * SLOT, SLOT // 2, BF16)
            return v

        KT = region(O_KT, 8 * 2304, BF16).rearrange("p (h k) -> p h k", h=8)
        V = region(O_V, 18 * 8 * 130, BF16).rearrange("p (c h e) -> p c h e", c=18, h=8)
        hT_own = region(O_HT, 16 * 1024, BF16).rearrange("p (c t) -> p c t", c=16)
        hT_halo = region(O_HALO, 16 * 16, BF16).rearrange("p (c t) -> p c t", c=16)
        xst = [region(O_X + i * 8192, 2048, F32) for i in range(2)]
        xn = region(O_X + 16384, 2048, BF16)
        hT_tmp = [region(O_X + 20480 + i * 4096, 2048, BF16).rearrange("p (c t) -> p c t", c=16) for i in range(2)]
        ropt = [region(O_X + 28672 + i * 512, 128, F32) for i in range(2)]
        t1 = region(O_X + 29696, 512, F32)
        krot = region(O_X + 31744, 512, BF16)
        assert O_X + 32768 <= ARENA
        QT = region(RING + 3 * SLOT, 8 * 1024, BF16).rearrange("p (h t) -> p h t", h=8)
        mixT = region(O_X, 16 * 1024, BF16).rearrange("p (c t) -> p c t", c=16)
        O_BT = RING + 2 * SLOT
        pu = region(O_BT, 1040, F32)
        psA = region(O_BT + 4160, 1040, F32)
        psB = region(O_BT + 8320, 1040, F32)
        t1b = region(O_BT + 12480, 512, F32)
        krotb = region(O_BT + 14528, 512, BF16)
        roptb = region(O_BT + 15552, 128, F32)
        assert 15552 + 512 <= SLOT
        O_MA = O_X + 16384
        dT2 = [region(O_MA + i_ * 4096, 2 * 1024, BF16).rearrange("p (c t) -> p c t", c=2) for i_ in range(2)]
        wpool = region(O_MA + 8192, 4 * 2 * 256, BF16).rearrange("p (g k n) -> p g k n", g=4, k=2)
        invc = region(O_MA + 12288, 64, F32)
        etmp = region(O_MA + 12288 + 256, 8, F32)
        pbuf = [region(O_HT + i * 2048, 1024, BF16) for i in range(3)]
        osb1 = region(O_HT + 6144, 8 * 129, F32).rearrange("p (a e) -> p a e", a=8)
        a4 = region(O_HT + 10272, 4 * 128, F32).rearrange("p (s e) -> p s e", s=4)
        atmp = region(O_HT + 12320, 128, F32)
        abf = region(O_HT + 12832, 4 * 128, BF16).rearrange("p (s e) -> p s e", s=4)
        gsub = sml2[:, 0:128]
        lamt = sml2[:, 128:384]
        ajunk = sml2[:, 384:512]
        O_GT = O_HT + 13856
        GT_m = region(O_GT, 2048, F32)
        GT_f = region(O_GT + 8192, 2048, F32)
        assert O_GT + 16384 <= O_X
        O_C2 = RING + 2 * SLOT
        crep = region(O_C2, 16 * 128, BF16).rearrange("p (c m) -> p c m", c=16)
        rowt = [region(O_C2 + 4096 + i * 4096, 1024, F32) for i in range(2)]
        bt_tmp = region(O_C2 + 12288, 512, F32)
        colacc = region(O_C2 + 14336, 64, F32)
        h2T = region(O_KT, 16 * 1024, BF16).rearrange("p (c t) -> p c t", c=16)
        xres = [region(O_KT + 32768 + i * 8192, 2048, F32) for i in range(2)]
        xw = region(O_KT + 49152, 2048, F32)
        xn2 = region(O_KT + 57344, 2048, BF16)
        act_offs = []
        o = O_KT + 32768
        while o + 2048 <= O_GT:
            act_offs.append(o)
            o += 2048
        o = O_GT + 16384
        while o + 2048 <= O_X:
            act_offs.append(o)
            o += 2048
        o = O_X
        while len(act_offs) < 44:
            act_offs.append(o)
            o += 2048
        assert o <= ARENA and len(act_offs) == 44, (o, len(act_offs))
        actT = [region(oo, 1024, BF16) for oo in act_offs]
        sgt = [region(RING + 3 * SLOT + i * 4096, 1024, F32) for i in range(2)]
        ybuf = [region(O_KT + i * 8192, 2048, F32) for i in range(4)] + \
               [region(RING + 2 * SLOT + i * 8192, 2048, F32) for i in range(4)]
        wdn = [region(RING + i * 11264, 11 * 512, BF16).rearrange("p (k n) -> p k n", k=11) for i in range(2)]
        xr = [region(RING + 22528, 2048, F32), region(O_GT, 2048, F32)]
        xrj = [region(RING + 22528, 2048, BF16), region(O_GT, 2048, BF16)]

        def ring16(s):
            return ring_slot(s).rearrange("p (k n) -> p k n", k=16)

        def act(out, in_, func, reads, writes, bias=None, scale=None, accum=None):
            kw = {}
            if bias is not None:
                kw["bias"] = bias
            if scale is not None:
                kw["scale"] = scale
            if accum is not None:
                kw["accum_out"] = accum
            return P.op("act", lambda e: e.activation(out, in_, func, **kw), reads, writes)

        def dve(fn, reads, writes):
            return P.op("dve", fn, reads, writes)

        def mm(out, lhsT, rhs, start, stop, reads, writes, signal, skip=False):
            return P.op("pe", lambda e: e.matmul(out, lhsT, rhs, start=start, stop=stop,
                                                 skip_group_check=skip),
                        reads, writes, signal=signal)

        def tp(out, in_, reads, writes, signal):
            return P.op("pe", lambda e: e.transpose(out, in_, ident[:]), reads, writes, signal=signal)

        def dma(q, out, in_, reads, writes):
            return P.dma(q, lambda e: e.dma_start(out=out, in_=in_), reads, writes)

        def bc_last(ap2, n):
            return ap2.to_broadcast([128, n])

        def rstd_chain(ss_ap, ln_ap, rs_ap, n_inv, key):
            act(ln_ap, ss_ap, AF.Ln, [("ss", key), "epsc"], [("ln", key)], bias=eps_c[:, 0:1], scale=n_inv)
            act(rs_ap, ln_ap, AF.Exp, [("ln", key)], [("rs", key)], scale=-0.5)

        NT = 19

        def hT_dest(i):
            if 2 <= i < 10:
                return hT_own[:, :, (i - 2) * 128:(i - 1) * 128], ("hTown", i - 2)
            if i == 18:
                return None, ("hThalo",)
            return hT_tmp[i % 2][:], ("hTtmp", i % 2)

        def hT_chain(xsrc_rows, xs, xnb, sskey, with_rope=None):
            dma("sp", xs, xsrc_rows, [], [("xst", id(xs))])
            if with_rope is not None:
                rbuf, ridx, rkey = with_rope
                dma("sp", rbuf, rope_d[ridx], [], [rkey])
            ss_ap, ln_ap, rs_ap, key = sskey
            act(xnb, xs, AF.Square, [("xst", id(xs))], [("xn", id(xnb)), ("ss", key)], accum=ss_ap)
            rstd_chain(ss_ap, ln_ap, rs_ap, 1.0 / D, key)
            act(xnb, xs, AF.Identity, [("xst", id(xs)), ("rs", key)], [("xn", id(xnb))], scale=rs_ap)

        def hT_tp(xnb, tps_b):
            tpv = bankbf(tps_b, 2)
            for kc in range(16):
                tp(tpv[:, kc * 128:(kc + 1) * 128], xnb[:, kc * 128:(kc + 1) * 128],
                   [("xn", id(xnb)), "ident"], [("bank", tps_b), ("bank", tps_b + 1)], signal=(kc == 15))

        def hT_evac(tps_b, gcol, shcol, dests):
            tpv = bankbf(tps_b, 2)
            for kc in range(16):
                for (dst, lo, hi, dkey) in dests:
                    dve(lambda e, kc=kc, dst=dst, lo=lo, hi=hi: e.tensor_scalar(
                        dst[:, kc, :], tpv[:, kc * 128 + lo: kc * 128 + hi],
                        modc[:, gcol + kc: gcol + kc + 1], modc[:, shcol + kc: shcol + kc + 1],
                        ALU.mult, ALU.add),
                        [("bank", tps_b), ("bank", tps_b + 1), ("modc", gcol), ("modc", shcol)], [(dkey, kc)])

        def rope_evac(pj, ropb, rkey, t1_, kr_, pjkey, t2bank, outkey):
            Cb = ropb[:, 0:64].unsqueeze(1).to_broadcast([128, 8, 64])
            dve(lambda e: e.tensor_tensor(t1_.rearrange("p (g d) -> p g d", g=8),
                                          pj.rearrange("p (g d) -> p g d", g=8), Cb, ALU.mult),
                [pjkey, rkey], [("t1", id(t1_))])
            t2 = bank(t2bank)
            for half in range(2):
                src = pj.rearrange("p (g a h j) -> p g a h j", g=8, a=2, h=2)[:, :, :, 1 - half, :]
                dstv = t2.rearrange("p (g a h j) -> p g a h j", g=8, a=2, h=2)[:, :, :, half, :]
                Sb = bass.AP(ropb.tensor, ropb.offset + 64 + half * 16, [list(ropb.ap[0]), [0, 8], [32, 2], [1, 16]])
                dve(lambda e, src=src, dstv=dstv, Sb=Sb: e.tensor_tensor(dstv, src, Sb, ALU.mult),
                    [pjkey, rkey], [("bank", t2bank)])
            dve(lambda e: e.tensor_tensor(kr_, t2, t1_, ALU.add),
                [("bank", t2bank), ("t1", id(t1_))], [outkey])

        def transpose_heads(kr_, krkey, dstT, dkey, tpbank, on_act=True):
            ktp = bankbf(tpbank)[:, 0:512]
            for hh in range(4):
                tp(ktp[:, hh * 128:(hh + 1) * 128], kr_[:, hh * 128:(hh + 1) * 128],
                   [krkey, "ident"], [("bank", tpbank)], signal=(hh == 3))
            src = ktp.rearrange("p (h t) -> p h t", h=4)
            if on_act:
                act(dstT, src, AF.Identity, [("bank", tpbank)], [dkey])
            else:
                dve(lambda e: e.tensor_copy(dstT, src), [("bank", tpbank)], [dkey])

        pjrot = [0]

        def a_chain(i):
            lat = 2 <= i < 18
            hT_chain(xa[i * 128:(i + 1) * 128, :], xst[i % 2], xn,
                     (sml[:, S_SS + i:S_SS + i + 1], sml[:, S_LN + i:S_LN + i + 1], sml[:, S_RS + i:S_RS + i + 1], ("A", i)),
                     with_rope=(ropt[i % 2], i - 2, ("ropt", i % 2)) if lat else None)

        def a_evac(i, tb_=3):
            dst, dkey = hT_dest(i)
            if i == 18:
                dests = [(hT_halo, 0, 16, ("hThalo",))]
            else:
                dests = [(dst, 0, 128, dkey)]
            gcol, shcol = (0, 16) if i >= 2 else (32, 48)
            hT_evac(tb_, gcol, shcol, dests)

        def a_mm(i, cb):
            dst, dkey = hT_dest(i)
            b = pjrot[0] % 3
            pjrot[0] += 1
            pj = bank(b)
            for kc in range(16):
                mm(pj, dst[:, kc, :], ring16(cb)[:, kc, :], kc == 0, kc == 15,
                   [(dkey, kc), ("ring", cb)], [("bank", b)], signal=(kc == 15))
            return b

        def a_krope(i, cb, b):
            if 2 <= i < 18:
                rope_evac(bank(b), ropt[i % 2], ("ropt", i % 2), t1, krot, ("bank", b), 6, "krot")
            else:
                act(krot, bank(b), AF.Identity, [("bank", b)], ["krot"])

        def a_ktp(i, cb):
            transpose_heads(krot, "krot", KT[:, 4 * cb:4 * cb + 4, i * 128:(i + 1) * 128], ("KT", i, cb), 5)

        def a_vevac(i, cb, b):
            c2 = cb - 2
            act(V[:, i, 4 * c2:4 * c2 + 4, 0:128], bank(b).rearrange("p (h e) -> p h e", h=4),
                AF.Identity, [("bank", b)], [("V", i, c2)])

        eps_c = sml[:, 252:253]
        P.op("dve", lambda e: e.memset(sml[:, 252:253], EPS), [], ["epsc"])
        dma("sp", cols[:], cols_d, [], ["cols"])
        dma("sp", ident[:], ident_d, [], ["ident"])
        act(sml[:, S_SC:S_SC + 32], cols[:, 0:32], AF.Silu, ["cols"], ["sc"])
        dve(lambda e: e.tensor_copy(cpair[:].rearrange("p k j -> p j k"),
                                    sml[:, S_SC:S_SC + 32].rearrange("p (j k) -> p j k", j=2)),
            ["sc"], ["cpair"])
        wslots = [region(O_KT + i_ * SLOT, SLOT // 2, BF16).rearrange("p (k n) -> p k n", k=16) for i_ in range(4)]
        assert O_KT + 4 * SLOT <= O_HT

        def load_wada_block(j, slot):
            dma("pool", wslots[slot][:], w_ada_v[:, :, j * 512:(j + 1) * 512], [], [("wab", slot)])

        pscols = bank(7)[:, 0:64].rearrange("p (f j) -> p f j", j=2)

        def col_block(j, slot, f0):
            blk = wslots[slot]
            for f in range(4):
                for kc in range(16):
                    mm(pscols[:, f0 + f, :], blk[:, kc, f * 128:(f + 1) * 128], cpair[:, kc, :],
                       kc == 0, kc == 15, [("wab", slot), "cpair"], [("bank", 7)],
                       signal=(kc == 15 and f == 3))

        for j in range(4):
            load_wada_block(j, j % 4)
        for i in range(3):
            a_chain(i)
            hT_tp(xn, 1 + 2 * i)
        for j in range(8):
            if j >= 4:
                load_wada_block(j, j % 4)
            col_block(j, j % 4, 4 * j)
        tmpc = sml[:, S_TMP:S_TMP + 32]
        for j, (o_sh, o_g) in enumerate(((16, 0), (48, 32))):
            dve(lambda e, j=j, o_sh=o_sh: e.tensor_tensor(modc[:, o_sh:o_sh + 16], pscols[:, 0:16, j],
                                                         cols[:, C_B0:C_B0 + 16], ALU.add),
                [("bank", 7), "cols"], [("modc", o_sh)])
            dve(lambda e, j=j: e.tensor_tensor(tmpc[:, j * 16:(j + 1) * 16], pscols[:, 16:32, j],
                                               cols[:, C_B1:C_B1 + 16], ALU.add),
                [("bank", 7), "cols"], [("tmpc", j)])
            dve(lambda e, j=j, o_g=o_g: e.scalar_tensor_tensor(modc[:, o_g:o_g + 16], tmpc[:, j * 16:(j + 1) * 16],
                                                              1.0, cols[:, C_GPM:C_GPM + 16], ALU.add, ALU.mult),
                [("tmpc", j), "cols"], [("modc", o_g)])
        d0_ = P.all_counts()
        for s_ in range(4):
            dma("pool", ring16(s_)[:], w_in_v[:, :, 2048 + s_ * 512: 2048 + (s_ + 1) * 512], [], [("ring", s_)])
        P.barrier_with(d0_, [("ring", s_) for s_ in range(4)])

        P.op("dve", lambda e: e.memset(V[:, :, :, 128:129], 1.0), [], ["Vones"])

        for i in range(3):
            a_evac(i, 1 + 2 * i)
        a_chain(3)
        prevk = None
        for cb in range(4):
            for i in range(4):
                b_ = a_mm(i, cb)
                if prevk is not None:
                    a_ktp(*prevk)
                    prevk = None
                if cb < 2:
                    a_krope(i, cb, b_)
                    prevk = (i, cb)
                else:
                    a_vevac(i, cb, b_)
                if cb == 0 and i == 1:
                    hT_tp(xn, 3)
                    a_evac(3)
        a_chain(4)
        hT_tp(xn, 3)
        a_evac(4)
        for i in range(4, 18):
            a_chain(i + 1)
            b0_ = a_mm(i, 0)
            a_krope(i, 0, b0_)
            b1_ = a_mm(i, 1)
            a_ktp(i, 0)
            a_krope(i, 1, b1_)
            hT_tp(xn, 3)
            b2_ = a_mm(i, 2)
            a_ktp(i, 1)
            a_vevac(i, 2, b2_)
            a_evac(i + 1)
            b3_ = a_mm(i, 3)
            a_vevac(i, 3, b3_)
        P.barrier()
        if DEBUG:
            dma("sp", dbg["hT"], region(O_HT, 16 * 1024, BF16), [], [])
            dma("sp", dbg["KT"], region(O_KT, 8 * 2304, BF16), [], [])
            dma("sp", dbg["V"], region(O_V, 18 * 8 * 130, BF16), [], [])
            dma("sp", dbg["modc"], modc[:], [], [])
            P.barrier()

        for s in range(2):
            dma("pool", ring16(s)[:], w_in_v[:, :, 1024 + s * 512: 1024 + (s + 1) * 512], [], [("ring", s)])
        dma("pool", wpool[:], w_pool_v, [], ["wpool"])
        dma("sp", invc, rows_d[R_INVC:R_INVC + 64].partition_broadcast(128), [], ["invc"])
        def load_pgrp(g):
            wp_ = ring_slot(g % 2)[:, 0:16 * 256].rearrange("p (k n) -> p k n", k=16)
            dma("pool", wp_, w_in_v[:, :, g * 256:(g + 1) * 256], [], [("ring", g % 2)])

        prevq = None
        for cb in range(2):
            for t in range(8):
                b = pjrot[0] % 3
                pjrot[0] += 1
                pj = bank(b)
                for kc in range(16):
                    mm(pj, hT_own[:, kc, t * 128:(t + 1) * 128], ring16(cb)[:, kc, :], kc == 0, kc == 15,
                       [("ring", cb)], [("bank", b)], signal=(kc == 15))
                if prevq is not None:
                    pt, pcb = prevq
                    transpose_heads(krotb, "krotb", QT[:, 4 * pcb:4 * pcb + 4, pt * 128:(pt + 1) * 128],
                                    ("QT", pt, pcb), 5)
                dma("sp", roptb, rope_d[t], [], ["roptb"])
                rope_evac(pj, roptb, "roptb", t1b, krotb, ("bank", b), 6, "krotb")
                prevq = (t, cb)
            load_pgrp(cb)
        pt, pcb = prevq
        transpose_heads(krotb, "krotb", QT[:, 4 * pcb:4 * pcb + 4, pt * 128:(pt + 1) * 128], ("QT", pt, pcb), 5)

        def pool_mm(g):
            for oc in range(2):
                for tb in range(2):
                    bb = 3 + (oc * 2 + tb) % 2
                    for k2 in range(2):
                        mm(bank(bb), wpool[:, g, k2, oc * 128:(oc + 1) * 128], dT2[g % 2][:, k2, tb * 512:(tb + 1) * 512],
                           k2 == 0, k2 == 1, [("dT", g % 2, 0), ("dT", g % 2, 1), "wpool"], [("bank", bb)], signal=(k2 == 1))
                    cidx = 2 * g + oc
                    act(mixT[:, cidx, tb * 512:(tb + 1) * 512], bank(bb), AF.Identity, [("bank", bb)],
                        [("mixT", cidx, tb)], scale=cols[:, C_PS + cidx:C_PS + cidx + 1])

        for g in range(4):
            w = 2 << g
            slot = g % 2
            wp = ring_slot(slot)[:, 0:16 * 256].rearrange("p (k n) -> p k n", k=16)
            for cc in range(2):
                c = 2 * g + cc
                for kc in range(16):
                    lw = wp[:, kc, cc * 128:(cc + 1) * 128]
                    for tb in range(2):
                        mm(bank(tb), lw, hT_own[:, kc, tb * 512:(tb + 1) * 512], kc == 0, kc == 15,
                           [("ring", slot)], [("bank", tb)], signal=False)
                    mm(bank(2)[:, 0:16], lw, hT_halo[:, kc, :], kc == 0, kc == 15,
                       [("ring", slot), (("hThalo",), kc)], [("bank", 2)], signal=(kc == 15))
                act(pu[:, 8:520], bank(0), AF.Identity, [("bank", 0)], ["pu"])
                act(pu[:, 520:1032], bank(1), AF.Identity, [("bank", 1)], ["pu"])
                dve(lambda e: e.tensor_scalar(pu[:, 0:8], bank(2)[:, 0:8], cols[:, C_HL:C_HL + 1], None, ALU.mult),
                    [("bank", 2)], ["pu_l"])
                dve(lambda e: e.tensor_scalar(pu[:, 1032:1040], bank(2)[:, 8:16], cols[:, C_HR:C_HR + 1], None, ALU.mult),
                    [("bank", 2)], ["pu_r"])
                cur, L, step = pu, 1040, 1
                bufs = [psA, psB]
                bi = 0
                keys_cur = ["pu", "pu_l", "pu_r"]
                while step < w:
                    nxt = bufs[bi]
                    dve(lambda e, cur=cur, nxt=nxt, L=L, step=step: e.tensor_tensor(
                        nxt[:, 0:L - step], cur[:, 0:L - step], cur[:, step:L], ALU.add),
                        keys_cur, [("psum", bi)])
                    keys_cur = [("psum", bi)]
                    cur = nxt
                    L -= step
                    step *= 2
                    bi ^= 1
                off = 8 - w // 2
                dve(lambda e, cur=cur, off=off, cc=cc, w=w, g=g: e.scalar_tensor_tensor(
                    dT2[g % 2][:, cc, :], cur[:, off:off + 1024], 1.0 / w, pu[:, 8:1032], ALU.mult, ALU.subtract),
                    keys_cur + ["pu"], [("dT", g % 2, cc)])
                for (lo_t, e0) in ((0, 0), (1016, 8)):
                    dve(lambda e, cur=cur, off=off, lo_t=lo_t, e0=e0, g=g: e.tensor_tensor(
                        etmp[:, 0:8], cur[:, off + lo_t: off + lo_t + 8],
                        invc[:, g * 16 + e0: g * 16 + e0 + 8], ALU.mult),
                        keys_cur + ["invc"], ["edge"])
                    dve(lambda e, lo_t=lo_t, cc=cc, g=g: e.tensor_tensor(
                        dT2[g % 2][:, cc, lo_t:lo_t + 8], etmp[:, 0:8], pu[:, 8 + lo_t: 16 + lo_t], ALU.subtract),
                        ["edge", "pu"], [("dT", g % 2, cc)])
                if cc == 0 and g > 0:
                    pool_mm(g - 1)
            if g + 2 < 4:
                load_pgrp(g + 2)
        pool_mm(3)
        P.barrier()
        if DEBUG:
            dma("sp", dbg["QT"], region(RING + 3 * SLOT, 8 * 1024, BF16), [], [])
            P.barrier()

        dma("sp", lamt, rows_d[R_LAM:R_LAM + 256].partition_broadcast(128), [], ["lamt"])
        dma("sp", gsub, rows_d[R_GSUB:R_GSUB + 128].partition_broadcast(128), [], ["gsub"])
        for j in range(2):
            dve(lambda e, j=j: e.tensor_tensor(ajunk[:, j * 64:(j + 1) * 64], lamt[:, j * 128:j * 128 + 64],
                                               lamt[:, j * 128 + 64:j * 128 + 128], ALU.mult),
                ["lamt"], [("lamp", j)])
            dve(lambda e, j=j: e.reduce_sum(sml[:, S_LAM + j:S_LAM + j + 1], ajunk[:, j * 64:(j + 1) * 64],
                                            axis=mybir.AxisListType.X),
                [("lamp", j)], [("lam", j)])
        act(sml[:, S_LAM + 2:S_LAM + 4], sml[:, S_LAM:S_LAM + 2], AF.Exp, [("lam", 0), ("lam", 1)], ["lame"])
        dve(lambda e: e.tensor_tensor(sml[:, S_LAM + 4:S_LAM + 5], sml[:, S_LAM + 2:S_LAM + 3],
                                      sml[:, S_LAM + 3:S_LAM + 4], ALU.subtract), ["lame"], ["lam4"])
        dve(lambda e: e.tensor_scalar(sml[:, S_LAM + 5:S_LAM + 6], sml[:, S_LAM + 4:S_LAM + 5],
                                      LAMBDA_INIT, -1.0, ALU.add, ALU.mult), ["lam4"], ["nlam"])
        dve(lambda e: e.tensor_scalar(gsub, gsub, 1.0 - LAMBDA_INIT, None, ALU.mult), ["gsub"], ["gsub"])
        nlam = sml[:, S_LAM + 5:S_LAM + 6]

        Oacc = ps[:, 2048:4096].rearrange("p (a c) -> p a c", a=8)
        OB = [("bank", 4), ("bank", 5), ("bank", 6), ("bank", 7)]
        _scv = sml[:, S_SC:S_SC + 16]
        dve(lambda e: e.tensor_copy(crep[:], _scv.unsqueeze(2).to_broadcast([128, 16, 128])), ["sc"], ["crep"])

        def load_wada2(j):
            dma("pool", ring16(j % 2)[:], w_ada_v[:, :, j * 512:(j + 1) * 512], [], [("ring", j % 2)])

        def c2_block(j, bx):
            slot = j % 2
            blk = ring16(slot)
            if j < 12 or j >= 20:
                GT, n0, rb, rg = (GT_m, (j - 8) * 512, R_B2, R_GPOM) if j < 12 else (GT_f, (j - 20) * 512, R_B5, R_GPOF)
                rt = rowt[j % 2]
                dma("sp", rt[:, 0:512], rows_d[rb + n0: rb + n0 + 512].partition_broadcast(128), [], [("rowt", j % 2, 0)])
                dma("sp", rt[:, 512:1024], rows_d[rg + n0: rg + n0 + 512].partition_broadcast(128), [], [("rowt", j % 2, 1)])
                for kc in range(16):
                    mm(bank(bx), crep[:, kc, :], blk[:, kc, :], kc == 0, kc == 15, [("ring", slot), "crep"],
                       [("bank", bx)], signal=(kc == 15))
                dve(lambda e: e.tensor_tensor(bt_tmp, bank(bx), rt[:, 0:512], ALU.add),
                    [("bank", bx), ("rowt", j % 2, 0)], ["bt_tmp"])
                dve(lambda e: e.tensor_tensor(GT[:, n0:n0 + 512], bt_tmp, rt[:, 512:1024], ALU.mult),
                    ["bt_tmp", ("rowt", j % 2, 1)], [("GT", id(GT), n0)])
            else:
                f0 = 4 * (j - 12)
                pc = bank(bx)[:, 0:8].rearrange("p (f j) -> p f j", j=2)
                for f in range(4):
                    for kc in range(16):
                        mm(pc[:, f, :], blk[:, kc, f * 128:(f + 1) * 128], cpair[:, kc, :],
                           kc == 0, kc == 15, [("ring", slot), "cpair"], [("bank", bx)],
                           signal=(kc == 15 and f == 3))
                dve(lambda e: e.tensor_copy(colacc[:, f0:f0 + 4], pc[:, :, 0]), [("bank", bx)], [("colacc", f0)])

        def epi_dve1():
            dve(lambda e: e.tensor_copy(osb1[:], Oacc[:, :, 0:129]), OB, ["osb"])
            rr = sml[:, S_RR:S_RR + 8]
            dve(lambda e: e.reciprocal(rr, osb1[:, :, 128]), ["osb"], ["rr"])
            dve(lambda e: e.tensor_scalar(rr[:, 4:8], rr[:, 4:8], nlam, None, ALU.mult), ["rr", "nlam"], ["rr2"])
            for s_ in range(4):
                dve(lambda e, s_=s_: e.tensor_scalar(atmp, osb1[:, 4 + s_, 0:128], rr[:, 4 + s_:5 + s_], None, ALU.mult),
                    ["osb", "rr2"], ["atmp"])
                dve(lambda e, s_=s_: e.scalar_tensor_tensor(a4[:, s_, :], osb1[:, s_, 0:128], rr[:, s_:s_ + 1], atmp,
                                                            ALU.mult, ALU.add),
                    ["osb", "rr", "atmp"], [("a4", s_)])
                dve(lambda e, s_=s_: e.tensor_tensor(ajunk, a4[:, s_, :], a4[:, s_, :], ALU.mult),
                    [("a4", s_)], ["ajunk"])
                dve(lambda e, s_=s_: e.reduce_sum(sml[:, S_SSA + s_:S_SSA + s_ + 1], ajunk, axis=mybir.AxisListType.X),
                    ["ajunk"], [("ssa", s_)])

        def epi_2():
            act(sml[:, S_LNA:S_LNA + 4], sml[:, S_SSA:S_SSA + 4], AF.Ln, [("ssa", s_) for s_ in range(4)], ["lna"],
                bias=eps_c[:, 0:1], scale=1.0 / 128)
            act(sml[:, S_RSA:S_RSA + 4], sml[:, S_LNA:S_LNA + 4], AF.Exp, ["lna"], ["rsa"], scale=-0.5)
            for s_ in range(4):
                dve(lambda e, s_=s_: e.scalar_tensor_tensor(abf[:, s_, :], a4[:, s_, :], sml[:, S_RSA + s_:S_RSA + s_ + 1],
                                                            gsub, ALU.mult, ALU.mult),
                    [("a4", s_), "rsa", "gsub"], [("abf", s_)])

        def epi_3(h_, qb_, bx):
            tpv3 = bankbf(bx)[:, 0:512]
            for s_ in range(4):
                tp(tpv3[:, s_ * 128:(s_ + 1) * 128], abf[:, s_, :], [("abf", s_), "ident"], [("bank", bx)], signal=(s_ == 3))
            dve(lambda e: e.tensor_copy(mixT[:, 8 + h_, qb_ * 512:(qb_ + 1) * 512], tpv3), [("bank", bx)],
                [("mixT", 8 + h_, qb_)])

        prot = [0]
        load_wada2(8)
        load_wada2(9)
        prev_it = None
        it = 0
        for h in range(8):
            for qb in range(2):
                def qk(kc):
                    sp_ = kc % 2
                    for m in range(2):
                        mm(bank(2 * sp_ + m), KT[m * 64:(m + 1) * 64, h, kc * 128:(kc + 1) * 128],
                           QT[m * 64:(m + 1) * 64, h, qb * 512:(qb + 1) * 512], True, True,
                           [], [("bank", 2 * sp_ + m)], signal=True)
                    return sp_

                def expv(sp_):
                    pb = prot[0] % 3
                    prot[0] += 1
                    act(pbuf[pb], bank(2 * sp_, 2), AF.Exp, [("bank", 2 * sp_), ("bank", 2 * sp_ + 1)], [("P", pb)],
                        scale=QSCALE)
                    return pb

                def pv(kc, pb):
                    for m in range(2):
                        for s_ in range(4):
                            a = m * 4 + s_
                            mm(Oacc[:, a, 0:129], pbuf[pb][:, m * 512 + s_ * 128: m * 512 + (s_ + 1) * 128],
                               V[:, kc, h, 0:129], (kc == 0 and a % 2 == 0), kc == 17, [("P", pb)], OB,
                               signal=(m == 1 and s_ == 3), skip=True)

                sp_ = qk(0)
                for kc in range(18):
                    pb = expv(sp_)
                    nxt_bank = 2 * ((kc + 1) % 2)
                    if kc == 2 and prev_it is not None:
                        epi_2()
                    if kc == 5 and prev_it is not None:
                        epi_3(prev_it[0], prev_it[1], nxt_bank)
                    if kc == 11:
                        c2_block(8 + it, nxt_bank)
                        if 8 + it + 2 < 24:
                            load_wada2(8 + it + 2)
                        else:
                            s_ = (8 + it) % 2
                            dma("pool", ring16(s_)[:], w_out_v[:, :, s_ * 512:(s_ + 1) * 512], [], [("ring", s_)])
                    if kc + 1 < 18:
                        sp_ = qk(kc + 1)
                    pv(kc, pb)
                epi_dve1()
                prev_it = (h, qb)
                it += 1
        epi_2()
        epi_3(prev_it[0], prev_it[1], 0)
        CK = [("colacc", 4 * i_) for i_ in range(8)]
        dve(lambda e: e.tensor_tensor(modc[:, 80:96], colacc[:, 0:16], cols[:, C_B3:C_B3 + 16], ALU.add),
            CK + ["cols"], [("modc", 80)])
        dve(lambda e: e.tensor_tensor(tmpc[:, 0:16], colacc[:, 16:32], cols[:, C_B4:C_B4 + 16], ALU.add),
            CK + ["cols"], [("tmpc", 0)])
        dve(lambda e: e.scalar_tensor_tensor(modc[:, 64:80], tmpc[:, 0:16], 1.0, cols[:, C_GPF:C_GPF + 16],
                                             ALU.add, ALU.mult), [("tmpc", 0), "cols"], [("modc", 64)])
        P.barrier()
        if DEBUG:
            dma("sp", dbg["mixT"], region(O_X, 16 * 1024, BF16), [], [])
            dma("sp", dbg["GT"], region(O_GT, 4096, F32), [], [])
            P.barrier()

        for s in range(2, 4):
            dma("pool", ring16(s)[:], w_out_v[:, :, s * 512:(s + 1) * 512], [], [("ring", s)])
        MIXB = [("bank", 0), ("bank", 1), ("bank", 2), ("bank", 3)]
        mixps = bank(0, 4)

        def d_mm(t):
            for nb in range(4):
                for kc in range(16):
                    mm(bank(nb), mixT[:, kc, t * 128:(t + 1) * 128], ring16(nb)[:, kc, :], kc == 0, kc == 15,
                       [("ring", nb)], [("bank", nb)], signal=(kc == 15))

        xwj = region(O_KT + 49152, 4096, BF16)
        XWK = [("xw", nb) for nb in range(4)]
        XN2K = [("xn2", nb) for nb in range(4)]

        def d_epi_a(t):
            dma("sp", xres[t % 2], xa[(2 + t) * 128:(3 + t) * 128, :], [], [("xres", t % 2)])
            for nb in range(4):
                act(xwj[:, nb * 512:(nb + 1) * 512], bank(nb), AF.Square, [("bank", nb)],
                    XWK + [("ssp", t, nb)], accum=sml2[:, 256 + t * 4 + nb: 257 + t * 4 + nb])
            dve(lambda e: e.reduce_sum(sml[:, S_SS2 + t:S_SS2 + t + 1], sml2[:, 256 + t * 4: 260 + t * 4],
                                       axis=mybir.AxisListType.X),
                [("ssp", t, nb) for nb in range(4)], [("ss", ("D", t))])
            rstd_chain(sml[:, S_SS2 + t:S_SS2 + t + 1], sml[:, S_LN2 + t:S_LN2 + t + 1], sml[:, S_RS2 + t:S_RS2 + t + 1],
                       1.0 / D, ("D", t))
            for nb in range(4):
                dve(lambda e, nb=nb: e.scalar_tensor_tensor(xw[:, nb * 512:(nb + 1) * 512], bank(nb),
                                                            sml[:, S_RS2 + t:S_RS2 + t + 1],
                                                            GT_m[:, nb * 512:(nb + 1) * 512], ALU.mult, ALU.mult),
                    [("bank", nb), ("rs", ("D", t))], [("xw", nb)])
            dve(lambda e: e.tensor_tensor(xres[t % 2], xw, xres[t % 2], ALU.add),
                XWK + [("xres", t % 2)], [("xres", t % 2)])
            dma("sp", xscr[t * 128:(t + 1) * 128, :], xres[t % 2], [("xres", t % 2)], [("xscr", t)])

        def d_epi_b(t):
            act(xn2, xres[t % 2], AF.Square, [("xres", t % 2)], XN2K + [("ss", ("D3", t))],
                accum=sml[:, S_SS3 + t:S_SS3 + t + 1])
            rstd_chain(sml[:, S_SS3 + t:S_SS3 + t + 1], sml[:, S_LN3 + t:S_LN3 + t + 1], sml[:, S_RS3 + t:S_RS3 + t + 1],
                       1.0 / D, ("D3", t))
            act(xn2, xres[t % 2], AF.Identity, [("xres", t % 2), ("rs", ("D3", t))], XN2K,
                scale=sml[:, S_RS3 + t:S_RS3 + t + 1])

        tpvD = bankbf(4, 2)

        def d_tp(t):
            for kc in range(16):
                tp(tpvD[:, kc * 128:(kc + 1) * 128], xn2[:, kc * 128:(kc + 1) * 128], XN2K + ["ident"],
                   [("bank", 4), ("bank", 5)], signal=(kc == 15))

        def d_evac(t):
            for kc in range(16):
                act(h2T[:, kc, t * 128:(t + 1) * 128], tpvD[:, kc * 128:(kc + 1) * 128], AF.Identity,
                    [("bank", 4), ("bank", 5), ("modc", 64), ("modc", 80)], [("h2T", t, kc)],
                    scale=modc[:, 64 + kc:65 + kc], bias=modc[:, 80 + kc:81 + kc])

        d_mm(0)
        for t in range(8):
            d_epi_a(t)
            if t + 1 < 8:
                d_mm(t + 1)
            if t >= 1:
                d_evac(t - 1)
            d_epi_b(t)
            d_tp(t)
        d_evac(7)
        P.barrier()
        if DEBUG:
            dma("sp", dbg["h2T"], region(O_KT + 16384, 16 * 1024, BF16), [], [])
            P.barrier()

        NBLK = 22
        for jb in range(NBLK):
            slot = jb % 3
            blk = ring_slot(slot).rearrange("p (k j n) -> p k j n", k=16, j=2)
            dma("pool", blk[:, :, 0, :], w_gu_v[:, :, jb * 256:(jb + 1) * 256], [], [("ring", slot, 0)])
            dma("pool", blk[:, :, 1, :], w_gu_v[:, :, DFF + jb * 256: DFF + (jb + 1) * 256], [], [("ring", slot, 1)])
            for fc in range(2):
                c = 2 * jb + fc
                pb = 4 * (c % 2)
                for j in range(2):
                    for kc in range(16):
                        lw = blk[:, kc, j, fc * 128:(fc + 1) * 128]
                        for tb in range(2):
                            mm(bank(pb + 2 * j + tb), lw, h2T[:, kc, tb * 512:(tb + 1) * 512], kc == 0, kc == 15,
                               [("ring", slot, j)], [("bank", pb + 2 * j), ("bank", pb + 2 * j + 1)], signal=(kc == 15 and tb == 1))
                sg = sgt[c % 2]
                act(sg, bank(pb, 2), AF.Silu, [("bank", pb), ("bank", pb + 1)], [("sg", c % 2)])
                dve(lambda e, sg=sg, pb=pb, c=c: e.tensor_tensor(actT[c], sg, bank(pb + 2, 2), ALU.mult),
                    [("sg", c % 2), ("bank", pb + 2), ("bank", pb + 3)], [("actT", c)])
        P.barrier()

        bi = 0
        for n in range(4):
            for kq in range(4):
                wb = wdn[bi % 2]
                dma("pool", wb[:], w_dn_v[:, kq * 11:(kq + 1) * 11, n * 512:(n + 1) * 512], [], [("wdn", bi % 2)])
                for t in range(8):
                    for kc in range(11):
                        mm(bank(t), actT[kq * 11 + kc][:, t * 128:(t + 1) * 128], wb[:, kc, :],
                           (kq == 0 and kc == 0), (kq == 3 and kc == 10), [("wdn", bi % 2)], [("bank", t)],
                           signal=(kc == 10))
                bi += 1
            for t in range(8):
                if t % 2 == 0:
                    act(ybuf[t][:, n * 512:(n + 1) * 512], bank(t), AF.Identity, [("bank", t)], [("y", t, n)])
                else:
                    dve(lambda e, t=t, n=n: e.tensor_copy(ybuf[t][:, n * 512:(n + 1) * 512], bank(t)),
                        [("bank", t)], [("y", t, n)])
                if n == 3:
                    yk = [("y", t, nn) for nn in range(4)]
                    act(xrj[t % 2], ybuf[t], AF.Square, yk, [("xr", t % 2), ("ss", ("F", t))],
                        accum=sml[:, S_SS4 + t:S_SS4 + t + 1])
                    dma("sp", xr[t % 2], xscr[t * 128:(t + 1) * 128, :], [("xscr", t)], [("xr", t % 2)])
                    rstd_chain(sml[:, S_SS4 + t:S_SS4 + t + 1], sml[:, S_LN4 + t:S_LN4 + t + 1],
                               sml[:, S_RS4 + t:S_RS4 + t + 1], 1.0 / D, ("F", t))
                    dve(lambda e, t=t: e.scalar_tensor_tensor(ybuf[t], ybuf[t], sml[:, S_RS4 + t:S_RS4 + t + 1], GT_f,
                                                              ALU.mult, ALU.mult),
                        yk + [("rs", ("F", t))], [("yw", t)])
                    dve(lambda e, t=t: e.tensor_tensor(xr[t % 2], ybuf[t], xr[t % 2], ALU.add),
                        [("yw", t), ("xr", t % 2)], [("xr", t % 2)])
                    dma("sp", out_d[t * 128:(t + 1) * 128, :], xr[t % 2], [("xr", t % 2)], [("out", t)])
        final = P.all_counts()

        with nc.Block() as block:
            def emit(eng, name):
                for waits, fn, inc in P.ops[name]:
                    for k, v in waits:
                        eng.wait_ge(sems[k], v)
                    ins = fn(eng)
                    if inc is not None:
                        ins.then_inc(sems[inc[0]], inc[1])
                if name == "sp":
                    for k, v in final.items():
                        eng.wait_ge(sems[k], v)

            @block.sync
            def _(e):
                emit(e, "sp")

            @block.scalar
            def _(e):
                emit(e, "act")

            @block.vector
            def _(e):
                emit(e, "dve")

            @block.gpsimd
            def _(e):
                emit(e, "pool")

            @block.tensor
            def _(e):
                emit(e, "pe")
    return nc


def _rope_tables():
    t = np.arange(SEQ)
    row = (t // 64).astype(np.float32)
    col = (t % 64).astype(np.float32)
    half = 16
    inv = (10000.0 ** (-np.arange(half, dtype=np.float32) / half)).astype(np.float32)
    ar = row[:, None] * inv
    ac = col[:, None] * inv
    cr, sr, cc, sc = np.cos(ar), np.sin(ar), np.cos(ac), np.sin(ac)
    C = np.concatenate([cr, cr, cc, cc], axis=1)
    S = np.concatenate([-sr, sr, -sc, sc], axis=1)
    tab = np.concatenate([C, S], axis=1).astype(np.float32)
    return tab.reshape(16, 128, 128)


def _col(v):
    return np.ascontiguousarray(v.reshape(-1, 128).T)


_NC_CACHE = {}


def kernel(x, c, ctx, c_ctx, w_ada, b_ada, g_pre_mix, g_post_mix, g_pre_ffn, g_post_ffn,
           w_in, w_pool, pool_scale, lambda_q1, lambda_k1, lambda_q2, lambda_k2, g_subln,
           w_out, w_gate_up, w_down):
    f32 = np.float32
    x = np.asarray(x, f32); c = np.asarray(c, f32); ctx = np.asarray(ctx, f32); c_ctx = np.asarray(c_ctx, f32)
    b_ada = np.asarray(b_ada, f32)[0]
    rope_abs = _rope_tables()
    ident = np.eye(128, dtype=f32).astype(ml_dtypes.bfloat16)
    shared = {
        "ident": ident,
        "w_ada": np.ascontiguousarray(np.asarray(w_ada, f32)[0]),
        "w_in": np.ascontiguousarray(np.asarray(w_in, f32)[0]),
        "w_pool": np.ascontiguousarray(np.asarray(w_pool, f32)[0]),
        "w_out": np.ascontiguousarray(np.asarray(w_out, f32)[0]),
        "w_gu": np.ascontiguousarray(np.asarray(w_gate_up, f32)[0]),
        "w_dn": np.ascontiguousarray(np.asarray(w_down, f32)[0]),
    }
    in_maps = []
    for core in range(NCORES):
        b, half = core // 2, core % 2
        lo, hi = half * NOWN, (half + 1) * NOWN
        olo, ohi = (1 - half) * NOWN, (2 - half) * NOWN
        halo = np.zeros((128, D), f32)
        if half == 1:
            halo[0:8] = x[b, lo - 8:lo]
        else:
            halo[8:16] = x[b, hi:hi + 8]
        xa = np.concatenate([ctx[b], x[b, lo:hi], x[b, olo:ohi], halo], axis=0)
        own_tiles = list(range(lo // 128, hi // 128))
        oth_tiles = list(range(olo // 128, ohi // 128))
        rope = np.ascontiguousarray(rope_abs[own_tiles + oth_tiles])
        cols = np.zeros((128, NCOL), f32)
        cols[:, C_C:C_C + 16] = _col(c[b])
        cols[:, C_CC:C_CC + 16] = _col(c_ctx)
        cols[:, C_B0:C_B0 + 16] = _col(b_ada[0:2048])
        cols[:, C_B1:C_B1 + 16] = _col(b_ada[2048:4096])
        cols[:, C_B3:C_B3 + 16] = _col(b_ada[6144:8192])
        cols[:, C_B4:C_B4 + 16] = _col(b_ada[8192:10240])
        cols[:, C_GPM:C_GPM + 16] = _col(np.asarray(g_pre_mix, f32)[0])
        cols[:, C_GPF:C_GPF + 16] = _col(np.asarray(g_pre_ffn, f32)[0])
        cols[:, C_PS:C_PS + 8] = _col(np.asarray(pool_scale, f32)[0])
        cols[:, C_HL] = 1.0 if half == 1 else 0.0
        cols[:, C_HR] = 1.0 if half == 0 else 0.0
        rows = np.zeros((NROW,), f32)
        rows[R_B2:R_B2 + 2048] = b_ada[4096:6144]
        rows[R_B5:R_B5 + 2048] = b_ada[10240:12288]
        rows[R_GPOM:R_GPOM + 2048] = np.asarray(g_post_mix, f32)[0]
        rows[R_GPOF:R_GPOF + 2048] = np.asarray(g_post_ffn, f32)[0]
        rows[R_GSUB:R_GSUB + 128] = np.asarray(g_subln, f32)[0]
        rows[R_LAM:R_LAM + 64] = np.asarray(lambda_q1, f32)[0]
        rows[R_LAM + 64:R_LAM + 128] = np.asarray(lambda_k1, f32)[0]
        rows[R_LAM + 128:R_LAM + 192] = np.asarray(lambda_q2, f32)[0]
        rows[R_LAM + 192:R_LAM + 256] = np.asarray(lambda_k2, f32)[0]
        for g, w in enumerate((2, 4, 8, 16)):
            for e in range(16):
                tl = e if e < 8 else 1016 + (e - 8)
                t = lo + tl
                cnt = min(t + w // 2, SEQ) - max(t - w // 2, 0)
                rows[R_INVC + g * 16 + e] = 1.0 / cnt
        m = dict(shared)
        m.update({"xa": np.ascontiguousarray(xa), "cols": cols, "rows": rows, "rope": rope})
        in_maps.append(m)

    if "nc" not in _NC_CACHE:
        _NC_CACHE["nc"] = build_nc()
    nc = _NC_CACHE["nc"]
    res = run_bass_kernel_spmd(nc, in_maps, core_ids=list(range(NCORES)))
    out = np.zeros((4, SEQ, D), f32)
    for core in range(NCORES):
        b, half = core // 2, core % 2
        out[b, half * NOWN:(half + 1) * NOWN] = np.asarray(res.results[core]["out"], f32)
    if DEBUG:
        kernel.dbg = res.results
    return out
```

```python
import math
import numpy as np
import ml_dtypes
import concourse.bass as bass
import concourse.mybir as mybir
from concourse.bass_utils import run_bass_kernel_spmd

F32 = mybir.dt.float32
BF16 = mybir.dt.bfloat16
AF = mybir.ActivationFunctionType
ALU = mybir.AluOpType

D = 2048
SEQ = 2048
CTX = 256
NOWN = 1024
DFF = 5632
EPS = 1e-6
NCORES = 8
LAMBDA_INIT = 0.8 - 0.6 * math.exp(-0.3 * 0)
QSCALE = 64 ** -0.5

C_C, C_CC, C_B0, C_B1, C_B3, C_B4, C_GPM, C_GPF, C_PS, C_HL, C_HR = 0, 16, 32, 48, 64, 80, 96, 112, 128, 136, 137
NCOL = 144
R_B2, R_B5, R_GPOM, R_GPOF, R_GSUB, R_LAM, R_INVC = 0, 2048, 4096, 6144, 8192, 8320, 8576
NROW = 8640

DEBUG = False


class Prog:
    CE = ("pe", "act", "dve", "pool")

    def __init__(self, n_dma_sp=12, n_dma_pool=4):
        self.ops = {e: [] for e in ("pe", "act", "dve", "pool", "sp")}
        self.cnt = {e: 0 for e in self.CE}
        self.unsig = {e: False for e in self.CE}
        self.res = {}
        self.waited = {e: {} for e in self.ops}
        self.pending = {e: {} for e in self.ops}
        self.dq = {"sp": [["dsp%d" % i, 0] for i in range(n_dma_sp)],
                   "pool": [["dpl%d" % i, 0] for i in range(n_dma_pool)]}
        self.rr = {"sp": 0, "pool": 0}

    @staticmethod
    def _merge(dst, src):
        for k, v in src.items():
            if dst.get(k, 0) < v:
                dst[k] = v

    def _deps(self, eng, reads, writes):
        raw, oth = {}, {}
        for k in reads:
            r = self.res.get(k)
            if r:
                self._merge(raw, r[0])
        for k in writes:
            r = self.res.get(k)
            if r:
                self._merge(oth, r[0])
                self._merge(oth, r[1])
        if eng == "pe":
            oth.pop(eng, None)
            raw.pop(eng, None)
        deps = dict(self.pending[eng])
        self.pending[eng] = {}
        self._merge(deps, raw)
        self._merge(deps, oth)
        return deps

    def _emit(self, eng, deps, fn, inc):
        waits = []
        w = self.waited[eng]
        for k, v in deps.items():
            if w.get(k, 0) < v:
                w[k] = v
                waits.append((k, v))
        self.ops[eng].append((waits, fn, inc))

    def _register(self, tok, reads, writes):
        for k in reads:
            r = self.res.setdefault(k, [{}, {}])
            self._merge(r[1], {tok[0]: tok[1]})
        for k in writes:
            self.res[k] = [{tok[0]: tok[1]}, {}]

    def op(self, eng, fn, reads=(), writes=(), signal=True):
        deps = self._deps(eng, reads, writes)
        if signal:
            self.cnt[eng] += 1
            tok = (eng, self.cnt[eng])
            self.unsig[eng] = False
            inc = (eng, 1)
        else:
            tok = (eng, self.cnt[eng] + 1)
            if reads or writes:
                self.unsig[eng] = True
            inc = None
        self._emit(eng, deps, fn, inc)
        self._register(tok, reads, writes)
        return tok

    def dma(self, q, fn, reads=(), writes=()):
        slot = self.dq[q][self.rr[q] % len(self.dq[q])]
        self.rr[q] += 1
        deps = self._deps(q, reads, writes)
        if slot[1] > 0:
            self._merge(deps, {slot[0]: slot[1]})
        slot[1] += 16
        tok = (slot[0], slot[1])
        self._emit(q, deps, fn, (slot[0], 16))
        self._register(tok, reads, writes)
        return tok

    def all_counts(self):
        d = {}
        for e in self.CE:
            assert not self.unsig[e], "engine %s has unsignaled tail" % e
            if self.cnt[e]:
                d[e] = self.cnt[e]
        for q in self.dq.values():
            for k, v in q:
                if v:
                    d[k] = v
        return d

    def barrier(self):
        d = self.all_counts()
        for e in self.ops:
            self._merge(self.pending[e], d)
        self.res = {}

    def barrier_with(self, d, keep_keys):
        kept = {k: self.res[k] for k in keep_keys if k in self.res}
        for e in self.ops:
            self._merge(self.pending[e], d)
        self.res = kept

    def barrier_keep(self, toks, keep_keys):
        d = self.all_counts()
        for k, v in toks:
            if d.get(k) == v:
                if v > 16:
                    d[k] = v - 16
                else:
                    del d[k]
        self.barrier_with(d, keep_keys)

    def sem_names(self):
        return list(self.CE) + [s[0] for q in self.dq.values() for s in q]


def build_nc():
    nc = bass.Bass("TRN2", target_bir_lowering=False)
    P = Prog()

    xa = nc.dram_tensor("xa", [19 * 128, D], F32, kind="ExternalInput").ap()
    cols_d = nc.dram_tensor("cols", [128, NCOL], F32, kind="ExternalInput").ap()
    rows_d = nc.dram_tensor("rows", [NROW], F32, kind="ExternalInput").ap()
    rope_d = nc.dram_tensor("rope", [16, 128, 128], F32, kind="ExternalInput").ap()
    ident_d = nc.dram_tensor("ident", [128, 128], BF16, kind="ExternalInput").ap()
    w_ada = nc.dram_tensor("w_ada", [D, 6 * D], F32, kind="ExternalInput").ap()
    w_in = nc.dram_tensor("w_in", [D, 4096], F32, kind="ExternalInput").ap()
    w_pool = nc.dram_tensor("w_pool", [4, 256, 256], F32, kind="ExternalInput").ap()
    w_out = nc.dram_tensor("w_out", [D, D], F32, kind="ExternalInput").ap()
    w_gu = nc.dram_tensor("w_gu", [D, 2 * DFF], F32, kind="ExternalInput").ap()
    w_dn = nc.dram_tensor("w_dn", [DFF, D], F32, kind="ExternalInput").ap()
    out_d = nc.dram_tensor("out", [NOWN, D], F32, kind="ExternalOutput").ap()
    xscr = nc.dram_tensor("xscr", [NOWN, D], F32, kind="Internal").ap()
    dbg = {}
    if DEBUG:
        dbg["hT"] = nc.dram_tensor("dbg_hT", [128, 16 * 1024], BF16, kind="ExternalOutput").ap()
        dbg["KT"] = nc.dram_tensor("dbg_KT", [128, 8 * 2304], BF16, kind="ExternalOutput").ap()
        dbg["V"] = nc.dram_tensor("dbg_V", [128, 18 * 8 * 130], BF16, kind="ExternalOutput").ap()
        dbg["QT"] = nc.dram_tensor("dbg_QT", [128, 8 * 1024], BF16, kind="ExternalOutput").ap()
        dbg["mixT"] = nc.dram_tensor("dbg_mixT", [128, 16 * 1024], BF16, kind="ExternalOutput").ap()
        dbg["modc"] = nc.dram_tensor("dbg_modc", [128, 96], F32, kind="ExternalOutput").ap()
        dbg["h2T"] = nc.dram_tensor("dbg_h2T", [128, 16 * 1024], BF16, kind="ExternalOutput").ap()
        dbg["GT"] = nc.dram_tensor("dbg_GT", [128, 2 * 2048], F32, kind="ExternalOutput").ap()

    w_ada_v = w_ada.rearrange("(kc p) n -> p kc n", p=128)
    w_in_v = w_in.rearrange("(kc p) n -> p kc n", p=128)
    w_out_v = w_out.rearrange("(kc p) n -> p kc n", p=128)
    w_gu_v = w_gu.rearrange("(kc p) n -> p kc n", p=128)
    w_dn_v = w_dn.rearrange("(kc p) n -> p kc n", p=128)
    w_pool_v = w_pool.rearrange("g (kc p) n -> p g kc n", p=128)

    ARENA = 206912
    import contextlib
    es = contextlib.ExitStack()
    with es:
        arena = es.enter_context(nc.sbuf_tensor("arena", [128, ARENA // 2], BF16))
        ps = es.enter_context(nc.psum_tensor("ps", [128, 4096], F32))
        ident = es.enter_context(nc.sbuf_tensor("identsb", [128, 128], BF16))
        cols = es.enter_context(nc.sbuf_tensor("colsb", [128, NCOL], F32))
        modc = es.enter_context(nc.sbuf_tensor("modcsb", [128, 96], F32))
        sml = es.enter_context(nc.sbuf_tensor("sml", [128, 256], F32))
        cpair = es.enter_context(nc.sbuf_tensor("cpair", [128, 16, 2], BF16))
        sml2 = es.enter_context(nc.sbuf_tensor("sml2", [128, 512], F32))
        sems = {n: es.enter_context(nc.semaphore(n)) for n in P.sem_names()}

        def region(off, nelem, dt):
            assert off % 4 == 0
            if dt == BF16:
                assert off + 2 * nelem <= ARENA, (off, nelem)
                return arena[:, off // 2: off // 2 + nelem]
            assert off + 4 * nelem <= ARENA, (off, nelem)
            return arena[:, off // 2: off // 2 + 2 * nelem].bitcast(F32)

        def bank(b, n=1):
            return ps[:, b * 512:(b + n) * 512]

        def bankbf(b, n=1):
            return ps[:, b * 512:(b + n) * 512].bitcast(BF16)

        S_SC = 0
        S_SS = 32
        S_LN = 52
        S_RS = 72
        S_LAM = 92
        S_RR = 100
        S_SSA = 108
        S_LNA = 112
        S_RSA = 116
        S_TMP = 120
        S_SS2 = 152
        S_LN2 = 160
        S_RS2 = 168
        S_SS3 = 176
        S_LN3 = 184
        S_RS3 = 192
        S_SS4 = 200
        S_LN4 = 208
        S_RS4 = 216

        RING = 0
        SLOT = 16384
        O_KT = 65536
        O_V = O_KT + 36864
        O_HT = O_V + 37440
        O_HALO = O_HT + 32768
        O_X = O_HALO + 512

        def ring_slot(s, shape_str=None, **kw):
            v = region(RING + s * SLOT, SLOT // 2, BF16)
            return v

        KT = region(O_KT, 8 * 2304, BF16).rearrange("p (h k) -> p h k", h=8)
        V = region(O_V, 18 * 8 * 130, BF16).rearrange("p (c h e) -> p c h e", c=18, h=8)
        hT_own = region(O_HT, 16 * 1024, BF16).rearrange("p (c t) -> p c t", c=16)
        hT_halo = region(O_HALO, 16 * 16, BF16).rearrange("p (c t) -> p c t", c=16)
        xst = [region(O_X + i * 8192, 2048, F32) for i in range(2)]
        xn = region(O_X + 16384, 2048, BF16)
        hT_tmp = [region(O_X + 20480 + i * 4096, 2048, BF16).rearrange("p (c t) -> p c t", c=16) for i in range(2)]
        ropt = [region(O_X + 28672 + i * 512, 128, F32) for i in range(2)]
        t1 = region(O_X + 29696, 512, F32)
        krot = region(O_X + 31744, 512, BF16)
        assert O_X + 32768 <= ARENA
        QT = region(RING + 3 * SLOT, 8 * 1024, BF16).rearrange("p (h t) -> p h t", h=8)
        mixT = region(O_X, 16 * 1024, BF16).rearrange("p (c t) -> p c t", c=16)
        O_BT = RING + 2 * SLOT
        pu = region(O_BT, 1040, F32)
        psA = region(O_BT + 4160, 1040, F32)
        psB = region(O_BT + 8320, 1040, F32)
        t1b = region(O_BT + 12480, 512, F32)
        krotb = region(O_BT + 14528, 512, BF16)
        roptb = region(O_BT + 15552, 128, F32)
        assert 15552 + 512 <= SLOT
        O_MA = O_X + 16384
        dT2 = [region(O_MA + i_ * 4096, 2 * 1024, BF16).rearrange("p (c t) -> p c t", c=2) for i_ in range(2)]
        wpool = region(O_MA + 8192, 4 * 2 * 256, BF16).rearrange("p (g k n) -> p g k n", g=4, k=2)
        invc = region(O_MA + 12288, 64, F32)
        etmp = region(O_MA + 12288 + 256, 8, F32)
        pbuf = [region(O_HT + i * 2048, 1024, BF16) for i in range(3)]
        osb1 = region(O_HT + 6144, 8 * 129, F32).rearrange("p (a e) -> p a e", a=8)
        a4 = region(O_HT + 10272, 4 * 128, F32).rearrange("p (s e) -> p s e", s=4)
        atmp = region(O_HT + 12320, 128, F32)
        abf = region(O_HT + 12832, 4 * 128, BF16).rearrange("p (s e) -> p s e", s=4)
        gsub = sml2[:, 0:128]
        lamt = sml2[:, 128:384]
        ajunk = sml2[:, 384:512]
        O_GT = O_HT + 13856
        GT_m = region(O_GT, 2048, F32)
        GT_f = region(O_GT + 8192, 2048, F32)
        assert O_GT + 16384 <= O_X
        O_C2 = RING + 2 * SLOT
        crep = region(O_C2, 16 * 128, BF16).rearrange("p (c m) -> p c m", c=16)
        rowt = [region(O_C2 + 4096 + i * 4096, 1024, F32) for i in range(2)]
        bt_tmp = region(O_C2 + 12288, 512, F32)
        colacc = region(O_C2 + 14336, 64, F32)
        h2T = region(O_KT, 16 * 1024, BF16).rearrange("p (c t) -> p c t", c=16)
        xres = [region(O_KT + 32768 + i * 8192, 2048, F32) for i in range(2)]
        xw = region(O_KT + 49152, 2048, F32)
        xn2 = region(O_KT + 57344, 2048, BF16)
        act_offs = []
        o = O_KT + 32768
        while o + 2048 <= O_GT:
            act_offs.append(o)
            o += 2048
        o = O_GT + 16384
        while o + 2048 <= O_X:
            act_offs.append(o)
            o += 2048
        o = O_X
        while len(act_offs) < 44:
            act_offs.append(o)
            o += 2048
        assert o <= ARENA and len(act_offs) == 44, (o, len(act_offs))
        actT = [region(oo, 1024, BF16) for oo in act_offs]
        sgt = [region(RING + 3 * SLOT + i * 4096, 1024, F32) for i in range(2)]
        ybuf = [region(O_KT + i * 8192, 2048, F32) for i in range(4)] + \
               [region(RING + 2 * SLOT + i * 8192, 2048, F32) for i in range(4)]
        wdn = [region(RING + i * 11264, 11 * 512, BF16).rearrange("p (k n) -> p k n", k=11) for i in range(2)]
        xr = [region(RING + 22528, 2048, F32), region(O_GT, 2048, F32)]
        xrj = [region(RING + 22528, 2048, BF16), region(O_GT, 2048, BF16)]

        def ring16(s):
            return ring_slot(s).rearrange("p (k n) -> p k n", k=16)

        def act(out, in_, func, reads, writes, bias=None, scale=None, accum=None):
            kw = {}
            if bias is not None:
                kw["bias"] = bias
            if scale is not None:
                kw["scale"] = scale
            if accum is not None:
                kw["accum_out"] = accum
            return P.op("act", lambda e: e.activation(out, in_, func, **kw), reads, writes)

        def dve(fn, reads, writes):
            return P.op("dve", fn, reads, writes)

        def mm(out, lhsT, rhs, start, stop, reads, writes, signal, skip=False):
            return P.op("pe", lambda e: e.matmul(out, lhsT, rhs, start=start, stop=stop,
                                                 skip_group_check=skip),
                        reads, writes, signal=signal)

        def tp(out, in_, reads, writes, signal):
            return P.op("pe", lambda e: e.transpose(out, in_, ident[:]), reads, writes, signal=signal)

        def dma(q, out, in_, reads, writes):
            return P.dma(q, lambda e: e.dma_start(out=out, in_=in_), reads, writes)

        def bc_last(ap2, n):
            return ap2.to_broadcast([128, n])

        def rstd_chain(ss_ap, ln_ap, rs_ap, n_inv, key):
            act(ln_ap, ss_ap, AF.Ln, [("ss", key), "epsc"], [("ln", key)], bias=eps_c[:, 0:1], scale=n_inv)
            act(rs_ap, ln_ap, AF.Exp, [("ln", key)], [("rs", key)], scale=-0.5)

        NT = 19

        def hT_dest(i):
            if 2 <= i < 10:
                return hT_own[:, :, (i - 2) * 128:(i - 1) * 128], ("hTown", i - 2)
            if i == 18:
                return None, ("hThalo",)
            return hT_tmp[i % 2][:], ("hTtmp", i % 2)

        def hT_chain(xsrc_rows, xs, xnb, sskey, with_rope=None):
            dma("sp", xs, xsrc_rows, [], [("xst", id(xs))])
            if with_rope is not None:
                rbuf, ridx, rkey = with_rope
                dma("sp", rbuf, rope_d[ridx], [], [rkey])
            ss_ap, ln_ap, rs_ap, key = sskey
            act(xnb, xs, AF.Square, [("xst", id(xs))], [("xn", id(xnb)), ("ss", key)], accum=ss_ap)
            rstd_chain(ss_ap, ln_ap, rs_ap, 1.0 / D, key)
            act(xnb, xs, AF.Identity, [("xst", id(xs)), ("rs", key)], [("xn", id(xnb))], scale=rs_ap)

        def hT_tp(xnb, tps_b):
            tpv = bankbf(tps_b, 2)
            for kc in range(16):
                tp(tpv[:, kc * 128:(kc + 1) * 128], xnb[:, kc * 128:(kc + 1) * 128],
                   [("xn", id(xnb)), "ident"], [("bank", tps_b), ("bank", tps_b + 1)], signal=(kc == 15))

        def hT_evac(tps_b, gcol, shcol, dests):
            tpv = bankbf(tps_b, 2)
            for kc in range(16):
                for (dst, lo, hi, dkey) in dests:
                    dve(lambda e, kc=kc, dst=dst, lo=lo, hi=hi: e.tensor_scalar(
                        dst[:, kc, :], tpv[:, kc * 128 + lo: kc * 128 + hi],
                        modc[:, gcol + kc: gcol + kc + 1], modc[:, shcol + kc: shcol + kc + 1],
                        ALU.mult, ALU.add),
                        [("bank", tps_b), ("bank", tps_b + 1), ("modc", gcol), ("modc", shcol)], [(dkey, kc)])

        def rope_evac(pj, ropb, rkey, t1_, kr_, pjkey, t2bank, outkey):
            Cb = ropb[:, 0:64].unsqueeze(1).to_broadcast([128, 8, 64])
            dve(lambda e: e.tensor_tensor(t1_.rearrange("p (g d) -> p g d", g=8),
                                          pj.rearrange("p (g d) -> p g d", g=8), Cb, ALU.mult),
                [pjkey, rkey], [("t1", id(t1_))])
            t2 = bank(t2bank)
            for half in range(2):
                src = pj.rearrange("p (g a h j) -> p g a h j", g=8, a=2, h=2)[:, :, :, 1 - half, :]
                dstv = t2.rearrange("p (g a h j) -> p g a h j", g=8, a=2, h=2)[:, :, :, half, :]
                Sb = bass.AP(ropb.tensor, ropb.offset + 64 + half * 16, [list(ropb.ap[0]), [0, 8], [32, 2], [1, 16]])
                dve(lambda e, src=src, dstv=dstv, Sb=Sb: e.tensor_tensor(dstv, src, Sb, ALU.mult),
                    [pjkey, rkey], [("bank", t2bank)])
            dve(lambda e: e.tensor_tensor(kr_, t2, t1_, ALU.add),
                [("bank", t2bank), ("t1", id(t1_))], [outkey])

        def transpose_heads(kr_, krkey, dstT, dkey, tpbank, on_act=True):
            ktp = bankbf(tpbank)[:, 0:512]
            for hh in range(4):
                tp(ktp[:, hh * 128:(hh + 1) * 128], kr_[:, hh * 128:(hh + 1) * 128],
                   [krkey, "ident"], [("bank", tpbank)], signal=(hh == 3))
            src = ktp.rearrange("p (h t) -> p h t", h=4)
            if on_act:
                act(dstT, src, AF.Identity, [("bank", tpbank)], [dkey])
            else:
                dve(lambda e: e.tensor_copy(dstT, src), [("bank", tpbank)], [dkey])

        pjrot = [0]

        def a_chain(i):
            lat = 2 <= i < 18
            hT_chain(xa[i * 128:(i + 1) * 128, :], xst[i % 2], xn,
                     (sml[:, S_SS + i:S_SS + i + 1], sml[:, S_LN + i:S_LN + i + 1], sml[:, S_RS + i:S_RS + i + 1], ("A", i)),
                     with_rope=(ropt[i % 2], i - 2, ("ropt", i % 2)) if lat else None)

        def a_evac(i, tb_=3):
            dst, dkey = hT_dest(i)
            if i == 18:
                dests = [(hT_halo, 0, 16, ("hThalo",))]
            else:
                dests = [(dst, 0, 128, dkey)]
            gcol, shcol = (0, 16) if i >= 2 else (32, 48)
            hT_evac(tb_, gcol, shcol, dests)

        def a_mm(i, cb):
            dst, dkey = hT_dest(i)
            b = pjrot[0] % 3
            pjrot[0] += 1
            pj = bank(b)
            for kc in range(16):
                mm(pj, dst[:, kc, :], ring16(cb)[:, kc, :], kc == 0, kc == 15,
                   [(dkey, kc), ("ring", cb)], [("bank", b)], signal=(kc == 15))
            return b

        def a_krope(i, cb, b):
            if 2 <= i < 18:
                rope_evac(bank(b), ropt[i % 2], ("ropt", i % 2), t1, krot, ("bank", b), 6, "krot")
            else:
                act(krot, bank(b), AF.Identity, [("bank", b)], ["krot"])

        def a_ktp(i, cb):
            transpose_heads(krot, "krot", KT[:, 4 * cb:4 * cb + 4, i * 128:(i + 1) * 128], ("KT", i, cb), 5)

        def a_vevac(i, cb, b):
            c2 = cb - 2
            act(V[:, i, 4 * c2:4 * c2 + 4, 0:128], bank(b).rearrange("p (h e) -> p h e", h=4),
                AF.Identity, [("bank", b)], [("V", i, c2)])

        eps_c = sml[:, 252:253]
        P.op("dve", lambda e: e.memset(sml[:, 252:253], EPS), [], ["epsc"])
        dma("sp", cols[:], cols_d, [], ["cols"])
        dma("sp", ident[:], ident_d, [], ["ident"])
        act(sml[:, S_SC:S_SC + 32], cols[:, 0:32], AF.Silu, ["cols"], ["sc"])
        dve(lambda e: e.tensor_copy(cpair[:].rearrange("p k j -> p j k"),
                                    sml[:, S_SC:S_SC + 32].rearrange("p (j k) -> p j k", j=2)),
            ["sc"], ["cpair"])
        wslots = [region(O_KT + i_ * SLOT, SLOT // 2, BF16).rearrange("p (k n) -> p k n", k=16) for i_ in range(4)]
        assert O_KT + 4 * SLOT <= O_HT

        def load_wada_block(j, slot):
            dma("pool", wslots[slot][:], w_ada_v[:, :, j * 512:(j + 1) * 512], [], [("wab", slot)])

        pscols = bank(7)[:, 0:64].rearrange("p (f j) -> p f j", j=2)

        def col_block(j, slot, f0):
            blk = wslots[slot]
            for f in range(4):
                for kc in range(16):
                    mm(pscols[:, f0 + f, :], blk[:, kc, f * 128:(f + 1) * 128], cpair[:, kc, :],
                       kc == 0, kc == 15, [("wab", slot), "cpair"], [("bank", 7)],
                       signal=(kc == 15 and f == 3))

        for j in range(4):
            load_wada_block(j, j % 4)
        for i in range(3):
            a_chain(i)
            hT_tp(xn, 1 + 2 * i)
        for j in range(8):
            if j >= 4:
                load_wada_block(j, j % 4)
            col_block(j, j % 4, 4 * j)
        tmpc = sml[:, S_TMP:S_TMP + 32]
        for j, (o_sh, o_g) in enumerate(((16, 0), (48, 32))):
            dve(lambda e, j=j, o_sh=o_sh: e.tensor_tensor(modc[:, o_sh:o_sh + 16], pscols[:, 0:16, j],
                                                         cols[:, C_B0:C_B0 + 16], ALU.add),
                [("bank", 7), "cols"], [("modc", o_sh)])
            dve(lambda e, j=j: e.tensor_tensor(tmpc[:, j * 16:(j + 1) * 16], pscols[:, 16:32, j],
                                               cols[:, C_B1:C_B1 + 16], ALU.add),
                [("bank", 7), "cols"], [("tmpc", j)])
            dve(lambda e, j=j, o_g=o_g: e.scalar_tensor_tensor(modc[:, o_g:o_g + 16], tmpc[:, j * 16:(j + 1) * 16],
                                                              1.0, cols[:, C_GPM:C_GPM + 16], ALU.add, ALU.mult),
                [("tmpc", j), "cols"], [("modc", o_g)])
        d0_ = P.all_counts()
        for s_ in range(4):
            dma("pool", ring16(s_)[:], w_in_v[:, :, 2048 + s_ * 512: 2048 + (s_ + 1) * 512], [], [("ring", s_)])
        P.barrier_with(d0_, [("ring", s_) for s_ in range(4)])

        P.op("dve", lambda e: e.memset(V[:, :, :, 128:129], 1.0), [], ["Vones"])

        for i in range(3):
            a_evac(i, 1 + 2 * i)
        a_chain(3)
        prevk = None
        for cb in range(4):
            for i in range(4):
                b_ = a_mm(i, cb)
                if prevk is not None:
                    a_ktp(*prevk)
                    prevk = None
                if cb < 2:
                    a_krope(i, cb, b_)
                    prevk = (i, cb)
                else:
                    a_vevac(i, cb, b_)
                if cb == 0 and i == 1:
                    hT_tp(xn, 3)
                    a_evac(3)
        a_chain(4)
        hT_tp(xn, 3)
        a_evac(4)
        qtoks = []
        for i in range(4, 18):
            a_chain(i + 1)
            b0_ = a_mm(i, 0)
            if i == 17:
                qtoks.append(dma("pool", ring16(0)[:], w_in_v[:, :, 1024:1536], [], [("ring", 0)]))
            a_krope(i, 0, b0_)
            b1_ = a_mm(i, 1)
            if i == 17:
                qtoks.append(dma("pool", ring16(1)[:], w_in_v[:, :, 1536:2048], [], [("ring", 1)]))
            a_ktp(i, 0)
            a_krope(i, 1, b1_)
            hT_tp(xn, 3)
            b2_ = a_mm(i, 2)
            a_ktp(i, 1)
            a_vevac(i, 2, b2_)
            a_evac(i + 1)
            b3_ = a_mm(i, 3)
            a_vevac(i, 3, b3_)
        P.barrier_keep(qtoks, [("ring", 0), ("ring", 1)])
        if DEBUG:
            dma("sp", dbg["hT"], region(O_HT, 16 * 1024, BF16), [], [])
            dma("sp", dbg["KT"], region(O_KT, 8 * 2304, BF16), [], [])
            dma("sp", dbg["V"], region(O_V, 18 * 8 * 130, BF16), [], [])
            dma("sp", dbg["modc"], modc[:], [], [])
            P.barrier()

        dma("pool", wpool[:], w_pool_v, [], ["wpool"])
        dma("sp", invc, rows_d[R_INVC:R_INVC + 64].partition_broadcast(128), [], ["invc"])
        def load_pgrp(g):
            wp_ = ring_slot(g % 2)[:, 0:16 * 256].rearrange("p (k n) -> p k n", k=16)
            dma("pool", wp_, w_in_v[:, :, g * 256:(g + 1) * 256], [], [("ring", g % 2)])

        prevq = None
        for cb in range(2):
            for t in range(8):
                b = pjrot[0] % 3
                pjrot[0] += 1
                pj = bank(b)
                for kc in range(16):
                    mm(pj, hT_own[:, kc, t * 128:(t + 1) * 128], ring16(cb)[:, kc, :], kc == 0, kc == 15,
                       [("ring", cb)], [("bank", b)], signal=(kc == 15))
                if prevq is not None:
                    pt, pcb = prevq
                    transpose_heads(krotb, "krotb", QT[:, 4 * pcb:4 * pcb + 4, pt * 128:(pt + 1) * 128],
                                    ("QT", pt, pcb), 5)
                dma("sp", roptb, rope_d[t], [], ["roptb"])
                rope_evac(pj, roptb, "roptb", t1b, krotb, ("bank", b), 6, "krotb")
                prevq = (t, cb)
            load_pgrp(cb)
        pt, pcb = prevq
        transpose_heads(krotb, "krotb", QT[:, 4 * pcb:4 * pcb + 4, pt * 128:(pt + 1) * 128], ("QT", pt, pcb), 5)

        def pool_mm(g):
            for oc in range(2):
                for tb in range(2):
                    bb = 3 + (oc * 2 + tb) % 2
                    for k2 in range(2):
                        mm(bank(bb), wpool[:, g, k2, oc * 128:(oc + 1) * 128], dT2[g % 2][:, k2, tb * 512:(tb + 1) * 512],
                           k2 == 0, k2 == 1, [("dT", g % 2, 0), ("dT", g % 2, 1), "wpool"], [("bank", bb)], signal=(k2 == 1))
                    cidx = 2 * g + oc
                    act(mixT[:, cidx, tb * 512:(tb + 1) * 512], bank(bb), AF.Identity, [("bank", bb)],
                        [("mixT", cidx, tb)], scale=cols[:, C_PS + cidx:C_PS + cidx + 1])

        for g in range(4):
            w = 2 << g
            slot = g % 2
            wp = ring_slot(slot)[:, 0:16 * 256].rearrange("p (k n) -> p k n", k=16)
            for cc in range(2):
                c = 2 * g + cc
                for kc in range(16):
                    lw = wp[:, kc, cc * 128:(cc + 1) * 128]
                    for tb in range(2):
                        mm(bank(tb), lw, hT_own[:, kc, tb * 512:(tb + 1) * 512], kc == 0, kc == 15,
                           [("ring", slot)], [("bank", tb)], signal=False)
                    mm(bank(2)[:, 0:16], lw, hT_halo[:, kc, :], kc == 0, kc == 15,
                       [("ring", slot), (("hThalo",), kc)], [("bank", 2)], signal=(kc == 15))
                act(pu[:, 8:520], bank(0), AF.Identity, [("bank", 0)], ["pu"])
                act(pu[:, 520:1032], bank(1), AF.Identity, [("bank", 1)], ["pu"])
                dve(lambda e: e.tensor_scalar(pu[:, 0:8], bank(2)[:, 0:8], cols[:, C_HL:C_HL + 1], None, ALU.mult),
                    [("bank", 2)], ["pu_l"])
                dve(lambda e: e.tensor_scalar(pu[:, 1032:1040], bank(2)[:, 8:16], cols[:, C_HR:C_HR + 1], None, ALU.mult),
                    [("bank", 2)], ["pu_r"])
                cur, L, step = pu, 1040, 1
                bufs = [psA, psB]
                bi = 0
                keys_cur = ["pu", "pu_l", "pu_r"]
                while step < w:
                    nxt = bufs[bi]
                    dve(lambda e, cur=cur, nxt=nxt, L=L, step=step: e.tensor_tensor(
                        nxt[:, 0:L - step], cur[:, 0:L - step], cur[:, step:L], ALU.add),
                        keys_cur, [("psum", bi)])
                    keys_cur = [("psum", bi)]
                    cur = nxt
                    L -= step
                    step *= 2
                    bi ^= 1
                off = 8 - w // 2
                dve(lambda e, cur=cur, off=off, cc=cc, w=w, g=g: e.scalar_tensor_tensor(
                    dT2[g % 2][:, cc, :], cur[:, off:off + 1024], 1.0 / w, pu[:, 8:1032], ALU.mult, ALU.subtract),
                    keys_cur + ["pu"], [("dT", g % 2, cc)])
                for (lo_t, e0) in ((0, 0), (1016, 8)):
                    dve(lambda e, cur=cur, off=off, lo_t=lo_t, e0=e0, g=g: e.tensor_tensor(
                        etmp[:, 0:8], cur[:, off + lo_t: off + lo_t + 8],
                        invc[:, g * 16 + e0: g * 16 + e0 + 8], ALU.mult),
                        keys_cur + ["invc"], ["edge"])
                    dve(lambda e, lo_t=lo_t, cc=cc, g=g: e.tensor_tensor(
                        dT2[g % 2][:, cc, lo_t:lo_t + 8], etmp[:, 0:8], pu[:, 8 + lo_t: 16 + lo_t], ALU.subtract),
                        ["edge", "pu"], [("dT", g % 2, cc)])
                if cc == 0 and g > 0:
                    pool_mm(g - 1)
            if g + 2 < 4:
                load_pgrp(g + 2)
        pool_mm(3)
        P.barrier()
        if DEBUG:
            dma("sp", dbg["QT"], region(RING + 3 * SLOT, 8 * 1024, BF16), [], [])
            P.barrier()

        dma("sp", lamt, rows_d[R_LAM:R_LAM + 256].partition_broadcast(128), [], ["lamt"])
        dma("sp", gsub, rows_d[R_GSUB:R_GSUB + 128].partition_broadcast(128), [], ["gsub"])
        for j in range(2):
            dve(lambda e, j=j: e.tensor_tensor(ajunk[:, j * 64:(j + 1) * 64], lamt[:, j * 128:j * 128 + 64],
                                               lamt[:, j * 128 + 64:j * 128 + 128], ALU.mult),
                ["lamt"], [("lamp", j)])
            dve(lambda e, j=j: e.reduce_sum(sml[:, S_LAM + j:S_LAM + j + 1], ajunk[:, j * 64:(j + 1) * 64],
                                            axis=mybir.AxisListType.X),
                [("lamp", j)], [("lam", j)])
        act(sml[:, S_LAM + 2:S_LAM + 4], sml[:, S_LAM:S_LAM + 2], AF.Exp, [("lam", 0), ("lam", 1)], ["lame"])
        dve(lambda e: e.tensor_tensor(sml[:, S_LAM + 4:S_LAM + 5], sml[:, S_LAM + 2:S_LAM + 3],
                                      sml[:, S_LAM + 3:S_LAM + 4], ALU.subtract), ["lame"], ["lam4"])
        dve(lambda e: e.tensor_scalar(sml[:, S_LAM + 5:S_LAM + 6], sml[:, S_LAM + 4:S_LAM + 5],
                                      LAMBDA_INIT, -1.0, ALU.add, ALU.mult), ["lam4"], ["nlam"])
        dve(lambda e: e.tensor_scalar(gsub, gsub, 1.0 - LAMBDA_INIT, None, ALU.mult), ["gsub"], ["gsub"])
        nlam = sml[:, S_LAM + 5:S_LAM + 6]

        Oacc = ps[:, 2048:4096].rearrange("p (a c) -> p a c", a=8)
        OB = [("bank", 4), ("bank", 5), ("bank", 6), ("bank", 7)]
        _scv = sml[:, S_SC:S_SC + 16]
        dve(lambda e: e.tensor_copy(crep[:], _scv.unsqueeze(2).to_broadcast([128, 16, 128])), ["sc"], ["crep"])

        def load_wada2(j):
            dma("pool", ring16(j % 2)[:], w_ada_v[:, :, j * 512:(j + 1) * 512], [], [("ring", j % 2)])

        def c2_block(j, bx):
            slot = j % 2
            blk = ring16(slot)
            if j < 12 or j >= 20:
                GT, n0, rb, rg = (GT_m, (j - 8) * 512, R_B2, R_GPOM) if j < 12 else (GT_f, (j - 20) * 512, R_B5, R_GPOF)
                rt = rowt[j % 2]
                dma("sp", rt[:, 0:512], rows_d[rb + n0: rb + n0 + 512].partition_broadcast(128), [], [("rowt", j % 2, 0)])
                dma("sp", rt[:, 512:1024], rows_d[rg + n0: rg + n0 + 512].partition_broadcast(128), [], [("rowt", j % 2, 1)])
                for kc in range(16):
                    mm(bank(bx), crep[:, kc, :], blk[:, kc, :], kc == 0, kc == 15, [("ring", slot), "crep"],
                       [("bank", bx)], signal=(kc == 15))
                dve(lambda e: e.tensor_tensor(bt_tmp, bank(bx), rt[:, 0:512], ALU.add),
                    [("bank", bx), ("rowt", j % 2, 0)], ["bt_tmp"])
                dve(lambda e: e.tensor_tensor(GT[:, n0:n0 + 512], bt_tmp, rt[:, 512:1024], ALU.mult),
                    ["bt_tmp", ("rowt", j % 2, 1)], [("GT", id(GT), n0)])
            else:
                f0 = 4 * (j - 12)
                pc = bank(bx)[:, 0:8].rearrange("p (f j) -> p f j", j=2)
                for f in range(4):
                    for kc in range(16):
                        mm(pc[:, f, :], blk[:, kc, f * 128:(f + 1) * 128], cpair[:, kc, :],
                           kc == 0, kc == 15, [("ring", slot), "cpair"], [("bank", bx)],
                           signal=(kc == 15 and f == 3))
                dve(lambda e: e.tensor_copy(colacc[:, f0:f0 + 4], pc[:, :, 0]), [("bank", bx)], [("colacc", f0)])

        def epi_dve1():
            dve(lambda e: e.tensor_copy(osb1[:], Oacc[:, :, 0:129]), OB, ["osb"])
            rr = sml[:, S_RR:S_RR + 8]
            dve(lambda e: e.reciprocal(rr, osb1[:, :, 128]), ["osb"], ["rr"])
            dve(lambda e: e.tensor_scalar(rr[:, 4:8], rr[:, 4:8], nlam, None, ALU.mult), ["rr", "nlam"], ["rr2"])
            for s_ in range(4):
                dve(lambda e, s_=s_: e.tensor_scalar(atmp, osb1[:, 4 + s_, 0:128], rr[:, 4 + s_:5 + s_], None, ALU.mult),
                    ["osb", "rr2"], ["atmp"])
                dve(lambda e, s_=s_: e.scalar_tensor_tensor(a4[:, s_, :], osb1[:, s_, 0:128], rr[:, s_:s_ + 1], atmp,
                                                            ALU.mult, ALU.add),
                    ["osb", "rr", "atmp"], [("a4", s_)])
                dve(lambda e, s_=s_: e.tensor_tensor(ajunk, a4[:, s_, :], a4[:, s_, :], ALU.mult),
                    [("a4", s_)], ["ajunk"])
                dve(lambda e, s_=s_: e.reduce_sum(sml[:, S_SSA + s_:S_SSA + s_ + 1], ajunk, axis=mybir.AxisListType.X),
                    ["ajunk"], [("ssa", s_)])

        def epi_2():
            act(sml[:, S_LNA:S_LNA + 4], sml[:, S_SSA:S_SSA + 4], AF.Ln, [("ssa", s_) for s_ in range(4)], ["lna"],
                bias=eps_c[:, 0:1], scale=1.0 / 128)
            act(sml[:, S_RSA:S_RSA + 4], sml[:, S_LNA:S_LNA + 4], AF.Exp, ["lna"], ["rsa"], scale=-0.5)
            for s_ in range(4):
                dve(lambda e, s_=s_: e.scalar_tensor_tensor(abf[:, s_, :], a4[:, s_, :], sml[:, S_RSA + s_:S_RSA + s_ + 1],
                                                            gsub, ALU.mult, ALU.mult),
                    [("a4", s_), "rsa", "gsub"], [("abf", s_)])

        def epi_3(h_, qb_, bx):
            tpv3 = bankbf(bx)[:, 0:512]
            for s_ in range(4):
                tp(tpv3[:, s_ * 128:(s_ + 1) * 128], abf[:, s_, :], [("abf", s_), "ident"], [("bank", bx)], signal=(s_ == 3))
            dve(lambda e: e.tensor_copy(mixT[:, 8 + h_, qb_ * 512:(qb_ + 1) * 512], tpv3), [("bank", bx)],
                [("mixT", 8 + h_, qb_)])

        prot = [0]
        load_wada2(8)
        load_wada2(9)
        prev_it = None
        it = 0
        for h in range(8):
            for qb in range(2):
                def qk(kc):
                    sp_ = kc % 2
                    for m in range(2):
                        mm(bank(2 * sp_ + m), KT[m * 64:(m + 1) * 64, h, kc * 128:(kc + 1) * 128],
                           QT[m * 64:(m + 1) * 64, h, qb * 512:(qb + 1) * 512], True, True,
                           [], [("bank", 2 * sp_ + m)], signal=True)
                    return sp_

                def expv(sp_):
                    pb = prot[0] % 3
                    prot[0] += 1
                    act(pbuf[pb], bank(2 * sp_, 2), AF.Exp, [("bank", 2 * sp_), ("bank", 2 * sp_ + 1)], [("P", pb)],
                        scale=QSCALE)
                    return pb

                def pv(kc, pb):
                    for m in range(2):
                        for s_ in range(4):
                            a = m * 4 + s_
                            mm(Oacc[:, a, 0:129], pbuf[pb][:, m * 512 + s_ * 128: m * 512 + (s_ + 1) * 128],
                               V[:, kc, h, 0:129], (kc == 0 and a % 2 == 0), kc == 17, [("P", pb)], OB,
                               signal=(m == 1 and s_ == 3), skip=True)

                sp_ = qk(0)
                for kc in range(18):
                    pb = expv(sp_)
                    nxt_bank = 2 * ((kc + 1) % 2)
                    if kc == 2 and prev_it is not None:
                        epi_2()
                    if kc == 5 and prev_it is not None:
                        epi_3(prev_it[0], prev_it[1], nxt_bank)
                    if kc == 11:
                        c2_block(8 + it, nxt_bank)
                        if 8 + it + 2 < 24:
                            load_wada2(8 + it + 2)
                        else:
                            s_ = (8 + it) % 2
                            dma("pool", ring16(s_)[:], w_out_v[:, :, s_ * 512:(s_ + 1) * 512], [], [("ring", s_)])
                    if kc + 1 < 18:
                        sp_ = qk(kc + 1)
                    pv(kc, pb)
                epi_dve1()
                prev_it = (h, qb)
                it += 1
        epi_2()
        epi_3(prev_it[0], prev_it[1], 0)
        CK = [("colacc", 4 * i_) for i_ in range(8)]
        dve(lambda e: e.tensor_tensor(modc[:, 80:96], colacc[:, 0:16], cols[:, C_B3:C_B3 + 16], ALU.add),
            CK + ["cols"], [("modc", 80)])
        dve(lambda e: e.tensor_tensor(tmpc[:, 0:16], colacc[:, 16:32], cols[:, C_B4:C_B4 + 16], ALU.add),
            CK + ["cols"], [("tmpc", 0)])
        dve(lambda e: e.scalar_tensor_tensor(modc[:, 64:80], tmpc[:, 0:16], 1.0, cols[:, C_GPF:C_GPF + 16],
                                             ALU.add, ALU.mult), [("tmpc", 0), "cols"], [("modc", 64)])
        P.barrier()
        if DEBUG:
            dma("sp", dbg["mixT"], region(O_X, 16 * 1024, BF16), [], [])
            dma("sp", dbg["GT"], region(O_GT, 4096, F32), [], [])
            P.barrier()

        for s in range(2, 4):
            dma("pool", ring16(s)[:], w_out_v[:, :, s * 512:(s + 1) * 512], [], [("ring", s)])
        MIXB = [("bank", 0), ("bank", 1), ("bank", 2), ("bank", 3)]
        mixps = bank(0, 4)

        def d_mm(t):
            for nb in range(4):
                for kc in range(16):
                    mm(bank(nb), mixT[:, kc, t * 128:(t + 1) * 128], ring16(nb)[:, kc, :], kc == 0, kc == 15,
                       [("ring", nb)], [("bank", nb)], signal=(kc == 15))

        xwj = region(O_KT + 49152, 4096, BF16)
        XWK = [("xw", nb) for nb in range(4)]
        XN2K = [("xn2", nb) for nb in range(4)]

        def d_epi_a(t):
            dma("sp", xres[t % 2], xa[(2 + t) * 128:(3 + t) * 128, :], [], [("xres", t % 2)])
            for nb in range(4):
                act(xwj[:, nb * 512:(nb + 1) * 512], bank(nb), AF.Square, [("bank", nb)],
                    XWK + [("ssp", t, nb)], accum=sml2[:, 256 + t * 4 + nb: 257 + t * 4 + nb])
            dve(lambda e: e.reduce_sum(sml[:, S_SS2 + t:S_SS2 + t + 1], sml2[:, 256 + t * 4: 260 + t * 4],
                                       axis=mybir.AxisListType.X),
                [("ssp", t, nb) for nb in range(4)], [("ss", ("D", t))])
            rstd_chain(sml[:, S_SS2 + t:S_SS2 + t + 1], sml[:, S_LN2 + t:S_LN2 + t + 1], sml[:, S_RS2 + t:S_RS2 + t + 1],
                       1.0 / D, ("D", t))
            for nb in range(4):
                dve(lambda e, nb=nb: e.scalar_tensor_tensor(xw[:, nb * 512:(nb + 1) * 512], bank(nb),
                                                            sml[:, S_RS2 + t:S_RS2 + t + 1],
                                                            GT_m[:, nb * 512:(nb + 1) * 512], ALU.mult, ALU.mult),
                    [("bank", nb), ("rs", ("D", t))], [("xw", nb)])
            dve(lambda e: e.tensor_tensor(xres[t % 2], xw, xres[t % 2], ALU.add),
                XWK + [("xres", t % 2)], [("xres", t % 2)])
            dma("sp", xscr[t * 128:(t + 1) * 128, :], xres[t % 2], [("xres", t % 2)], [("xscr", t)])

        def d_epi_b(t):
            act(xn2, xres[t % 2], AF.Square, [("xres", t % 2)], XN2K + [("ss", ("D3", t))],
                accum=sml[:, S_SS3 + t:S_SS3 + t + 1])
            rstd_chain(sml[:, S_SS3 + t:S_SS3 + t + 1], sml[:, S_LN3 + t:S_LN3 + t + 1], sml[:, S_RS3 + t:S_RS3 + t + 1],
                       1.0 / D, ("D3", t))
            act(xn2, xres[t % 2], AF.Identity, [("xres", t % 2), ("rs", ("D3", t))], XN2K,
                scale=sml[:, S_RS3 + t:S_RS3 + t + 1])

        tpvD = bankbf(4, 2)

        def d_tp(t):
            for kc in range(16):
                tp(tpvD[:, kc * 128:(kc + 1) * 128], xn2[:, kc * 128:(kc + 1) * 128], XN2K + ["ident"],
                   [("bank", 4), ("bank", 5)], signal=(kc == 15))

        def d_evac(t):
            for kc in range(16):
                act(h2T[:, kc, t * 128:(t + 1) * 128], tpvD[:, kc * 128:(kc + 1) * 128], AF.Identity,
                    [("bank", 4), ("bank", 5), ("modc", 64), ("modc", 80)], [("h2T", t, kc)],
                    scale=modc[:, 64 + kc:65 + kc], bias=modc[:, 80 + kc:81 + kc])

        gutoks = []
        d_mm(0)
        for t in range(8):
            d_epi_a(t)
            if t + 1 < 8:
                d_mm(t + 1)
                if t + 1 == 7:
                    for jb_ in range(3):
                        blk_ = ring_slot(jb_).rearrange("p (k j n) -> p k j n", k=16, j=2)
                        gutoks.append(dma("pool", blk_[:, :, 0, :], w_gu_v[:, :, jb_ * 256:(jb_ + 1) * 256], [],
                                          [("ring", jb_, 0), ("ring", jb_)]))
                        gutoks.append(dma("pool", blk_[:, :, 1, :], w_gu_v[:, :, DFF + jb_ * 256: DFF + (jb_ + 1) * 256], [],
                                          [("ring", jb_, 1), ("ring", jb_)]))
            if t >= 1:
                d_evac(t - 1)
            d_epi_b(t)
            d_tp(t)
        d_evac(7)
        P.barrier_keep(gutoks, [("ring", j_, k_) for j_ in range(3) for k_ in range(2)])
        if DEBUG:
            dma("sp", dbg["h2T"], region(O_KT + 16384, 16 * 1024, BF16), [], [])
            P.barrier()

        NBLK = 22
        for jb in range(NBLK):
            slot = jb % 3
            blk = ring_slot(slot).rearrange("p (k j n) -> p k j n", k=16, j=2)
            if jb >= 3:
                dma("pool", blk[:, :, 0, :], w_gu_v[:, :, jb * 256:(jb + 1) * 256], [], [("ring", slot, 0)])
                dma("pool", blk[:, :, 1, :], w_gu_v[:, :, DFF + jb * 256: DFF + (jb + 1) * 256], [], [("ring", slot, 1)])
            for fc in range(2):
                c = 2 * jb + fc
                pb = 4 * (c % 2)
                for j in range(2):
                    for kc in range(16):
                        lw = blk[:, kc, j, fc * 128:(fc + 1) * 128]
                        for tb in range(2):
                            mm(bank(pb + 2 * j + tb), lw, h2T[:, kc, tb * 512:(tb + 1) * 512], kc == 0, kc == 15,
                               [("ring", slot, j)], [("bank", pb + 2 * j), ("bank", pb + 2 * j + 1)], signal=(kc == 15 and tb == 1))
                sg = sgt[c % 2]
                act(sg, bank(pb, 2), AF.Silu, [("bank", pb), ("bank", pb + 1)], [("sg", c % 2)])
                dve(lambda e, sg=sg, pb=pb, c=c: e.tensor_tensor(actT[c], sg, bank(pb + 2, 2), ALU.mult),
                    [("sg", c % 2), ("bank", pb + 2), ("bank", pb + 3)], [("actT", c)])
        P.barrier()

        bi = 0
        for n in range(4):
            for kq in range(4):
                wb = wdn[bi % 2]
                dma("pool", wb[:], w_dn_v[:, kq * 11:(kq + 1) * 11, n * 512:(n + 1) * 512], [], [("wdn", bi % 2)])
                for t in range(8):
                    for kc in range(11):
                        mm(bank(t), actT[kq * 11 + kc][:, t * 128:(t + 1) * 128], wb[:, kc, :],
                           (kq == 0 and kc == 0), (kq == 3 and kc == 10), [("wdn", bi % 2)], [("bank", t)],
                           signal=(kc == 10))
                bi += 1
            for t in range(8):
                if t % 2 == 0:
                    act(ybuf[t][:, n * 512:(n + 1) * 512], bank(t), AF.Identity, [("bank", t)], [("y", t, n)])
                else:
                    dve(lambda e, t=t, n=n: e.tensor_copy(ybuf[t][:, n * 512:(n + 1) * 512], bank(t)),
                        [("bank", t)], [("y", t, n)])
                if n == 3:
                    yk = [("y", t, nn) for nn in range(4)]
                    act(xrj[t % 2], ybuf[t], AF.Square, yk, [("xr", t % 2), ("ss", ("F", t))],
                        accum=sml[:, S_SS4 + t:S_SS4 + t + 1])
                    dma("sp", xr[t % 2], xscr[t * 128:(t + 1) * 128, :], [("xscr", t)], [("xr", t % 2)])
                    rstd_chain(sml[:, S_SS4 + t:S_SS4 + t + 1], sml[:, S_LN4 + t:S_LN4 + t + 1],
                               sml[:, S_RS4 + t:S_RS4 + t + 1], 1.0 / D, ("F", t))
                    dve(lambda e, t=t: e.scalar_tensor_tensor(ybuf[t], ybuf[t], sml[:, S_RS4 + t:S_RS4 + t + 1], GT_f,
                                                              ALU.mult, ALU.mult),
                        yk + [("rs", ("F", t))], [("yw", t)])
                    dve(lambda e, t=t: e.tensor_tensor(xr[t % 2], ybuf[t], xr[t % 2], ALU.add),
                        [("yw", t), ("xr", t % 2)], [("xr", t % 2)])
                    dma("sp", out_d[t * 128:(t + 1) * 128, :], xr[t % 2], [("xr", t % 2)], [("out", t)])
        final = P.all_counts()

        with nc.Block() as block:
            def emit(eng, name):
                for waits, fn, inc in P.ops[name]:
                    for k, v in waits:
                        eng.wait_ge(sems[k], v)
                    ins = fn(eng)
                    if inc is not None:
                        ins.then_inc(sems[inc[0]], inc[1])
                if name == "sp":
                    for k, v in final.items():
                        eng.wait_ge(sems[k], v)

            @block.sync
            def _(e):
                emit(e, "sp")

            @block.scalar
            def _(e):
                emit(e, "act")

            @block.vector
            def _(e):
                emit(e, "dve")

            @block.gpsimd
            def _(e):
                emit(e, "pool")

            @block.tensor
            def _(e):
                emit(e, "pe")
    return nc


def _rope_tables():
    t = np.arange(SEQ)
    row = (t // 64).astype(np.float32)
    col = (t % 64).astype(np.float32)
    half = 16
    inv = (10000.0 ** (-np.arange(half, dtype=np.float32) / half)).astype(np.float32)
    ar = row[:, None] * inv
    ac = col[:, None] * inv
    cr, sr, cc, sc = np.cos(ar), np.sin(ar), np.cos(ac), np.sin(ac)
    C = np.concatenate([cr, cr, cc, cc], axis=1)
    S = np.concatenate([-sr, sr, -sc, sc], axis=1)
    tab = np.concatenate([C, S], axis=1).astype(np.float32)
    return tab.reshape(16, 128, 128)


def _col(v):
    return np.ascontiguousarray(v.reshape(-1, 128).T)


_NC_CACHE = {}


def kernel(x, c, ctx, c_ctx, w_ada, b_ada, g_pre_mix, g_post_mix, g_pre_ffn, g_post_ffn,
           w_in, w_pool, pool_scale, lambda_q1, lambda_k1, lambda_q2, lambda_k2, g_subln,
           w_out, w_gate_up, w_down):
    f32 = np.float32
    x = np.asarray(x, f32); c = np.asarray(c, f32); ctx = np.asarray(ctx, f32); c_ctx = np.asarray(c_ctx, f32)
    b_ada = np.asarray(b_ada, f32)[0]
    rope_abs = _rope_tables()
    ident = np.eye(128, dtype=f32).astype(ml_dtypes.bfloat16)
    shared = {
        "ident": ident,
        "w_ada": np.ascontiguousarray(np.asarray(w_ada, f32)[0]),
        "w_in": np.ascontiguousarray(np.asarray(w_in, f32)[0]),
        "w_pool": np.ascontiguousarray(np.asarray(w_pool, f32)[0]),
        "w_out": np.ascontiguousarray(np.asarray(w_out, f32)[0]),
        "w_gu": np.ascontiguousarray(np.asarray(w_gate_up, f32)[0]),
        "w_dn": np.ascontiguousarray(np.asarray(w_down, f32)[0]),
    }
    in_maps = []
    for core in range(NCORES):
        b, half = core // 2, core % 2
        lo, hi = half * NOWN, (half + 1) * NOWN
        olo, ohi = (1 - half) * NOWN, (2 - half) * NOWN
        halo = np.zeros((128, D), f32)
        if half == 1:
            halo[0:8] = x[b, lo - 8:lo]
        else:
            halo[8:16] = x[b, hi:hi + 8]
        xa = np.concatenate([ctx[b], x[b, lo:hi], x[b, olo:ohi], halo], axis=0)
        own_tiles = list(range(lo // 128, hi // 128))
        oth_tiles = list(range(olo // 128, ohi // 128))
        rope = np.ascontiguousarray(rope_abs[own_tiles + oth_tiles])
        cols = np.zeros((128, NCOL), f32)
        cols[:, C_C:C_C + 16] = _col(c[b])
        cols[:, C_CC:C_CC + 16] = _col(c_ctx)
        cols[:, C_B0:C_B0 + 16] = _col(b_ada[0:2048])
        cols[:, C_B1:C_B1 + 16] = _col(b_ada[2048:4096])
        cols[:, C_B3:C_B3 + 16] = _col(b_ada[6144:8192])
        cols[:, C_B4:C_B4 + 16] = _col(b_ada[8192:10240])
        cols[:, C_GPM:C_GPM + 16] = _col(np.asarray(g_pre_mix, f32)[0])
        cols[:, C_GPF:C_GPF + 16] = _col(np.asarray(g_pre_ffn, f32)[0])
        cols[:, C_PS:C_PS + 8] = _col(np.asarray(pool_scale, f32)[0])
        cols[:, C_HL] = 1.0 if half == 1 else 0.0
        cols[:, C_HR] = 1.0 if half == 0 else 0.0
        rows = np.zeros((NROW,), f32)
        rows[R_B2:R_B2 + 2048] = b_ada[4096:6144]
        rows[R_B5:R_B5 + 2048] = b_ada[10240:12288]
        rows[R_GPOM:R_GPOM + 2048] = np.asarray(g_post_mix, f32)[0]
        rows[R_GPOF:R_GPOF + 2048] = np.asarray(g_post_ffn, f32)[0]
        rows[R_GSUB:R_GSUB + 128] = np.asarray(g_subln, f32)[0]
        rows[R_LAM:R_LAM + 64] = np.asarray(lambda_q1, f32)[0]
        rows[R_LAM + 64:R_LAM + 128] = np.asarray(lambda_k1, f32)[0]
        rows[R_LAM + 128:R_LAM + 192] = np.asarray(lambda_q2, f32)[0]
        rows[R_LAM + 192:R_LAM + 256] = np.asarray(lambda_k2, f32)[0]
        for g, w in enumerate((2, 4, 8, 16)):
            for e in range(16):
                tl = e if e < 8 else 1016 + (e - 8)
                t = lo + tl
                cnt = min(t + w // 2, SEQ) - max(t - w // 2, 0)
                rows[R_INVC + g * 16 + e] = 1.0 / cnt
        m = dict(shared)
        m.update({"xa": np.ascontiguousarray(xa), "cols": cols, "rows": rows, "rope": rope})
        in_maps.append(m)

    if "nc" not in _NC_CACHE:
        _NC_CACHE["nc"] = build_nc()
    nc = _NC_CACHE["nc"]
    res = run_bass_kernel_spmd(nc, in_maps, core_ids=list(range(NCORES)))
    out = np.zeros((4, SEQ, D), f32)
    for core in range(NCORES):
        b, half = core // 2, core % 2
        out[b, half * NOWN:(half + 1) * NOWN] = np.asarray(res.results[core]["out"], f32)
    if DEBUG:
        kernel.dbg = res.results
    return out
```
